# Optimizing a Trainium2 kernel written in Bass

```python
import jax, jax.numpy as jnp
from jax import lax
import numpy as np

D_MODEL = 1024
BATCH = 8
SEQ = 2048
DEPTH = 1
DEC_BATCH = 128
DEC_SEQ = 8
PAST_LEN = 16384
PAGE_SIZE = 128

CONV_WIDTH = D_MODEL
CONV_W = 3
POOL_WIDTH = D_MODEL
POOL_WINDOWS = (2, 4, 8, 16)
N_POOL_GROUPS = len(POOL_WINDOWS)
POOL_GW = POOL_WIDTH // N_POOL_GROUPS
POOL_GW_OUT = D_MODEL // N_POOL_GROUPS
POOL_BUF = max(POOL_WINDOWS) - 1
D_FF = 4 * D_MODEL
PLE_DIM = 256
EPS = 1e-6
IN_COLS = 3 * CONV_WIDTH + POOL_WIDTH + 2 * D_MODEL

kernel_name = "hybrid_shortconv_pool_decoder_step"


def rms_norm(x, g):
    xf = x.astype(jnp.float32)
    xf = xf * lax.rsqrt(jnp.mean(xf * xf, axis=-1, keepdims=True) + EPS)
    return xf.astype(x.dtype) * g


def gated_short_conv(b, c, h, buf, w_conv):
    u = c * h
    ext = jnp.concatenate([buf, u], axis=1)
    T = u.shape[1]
    y = sum(ext[:, k:k + T] * w_conv[k] for k in range(CONV_W))
    return b * y, ext[:, -(CONV_W - 1):]


def multiscale_pool(v, buf, pos0, w_pool, pool_scale):
    Bn, T, _ = v.shape
    ext = jnp.concatenate([buf, v], axis=1)
    cs = jnp.cumsum(ext.astype(jnp.float32), axis=1)
    cs0 = jnp.concatenate([jnp.zeros((Bn, 1, POOL_WIDTH), jnp.float32), cs], axis=1)
    end = cs0[:, POOL_BUF + 1:POOL_BUF + 1 + T]
    pos = (pos0 + jnp.arange(T)).astype(jnp.float32)
    outs = []
    for gi, w in enumerate(POOL_WINDOWS):
        sl = slice(gi * POOL_GW, (gi + 1) * POOL_GW)
        s = end[..., sl] - cs0[:, POOL_BUF + 1 - w:POOL_BUF + 1 - w + T, sl]
        cnt = jnp.minimum(jnp.float32(w), pos + 1.0)[None, :, None]
        outs.append(s / cnt)
    pooled = jnp.concatenate(outs, axis=-1) - v.astype(jnp.float32)
    pooled = pooled.astype(v.dtype).reshape(Bn, T, N_POOL_GROUPS, POOL_GW)
    y = jnp.einsum('btgc,gcd->btgd', pooled, w_pool).reshape(Bn, T, D_MODEL)
    return y * pool_scale, ext[:, -POOL_BUF:]


def trunk(x, p, conv_bufs, pool_bufs, pos0, g_mix, w_in, w_conv, w_out_conv, w_pool,
          pool_scale, w_o, g_mlp, w_up, w_down, g_ple, w_ple_gate, w_ple_proj, g_final):
    h = x
    new_conv, new_pool = [], []
    o1, o2, o3 = CONV_WIDTH, 2 * CONV_WIDTH, 3 * CONV_WIDTH
    o4 = o3 + POOL_WIDTH
    o5 = o4 + D_MODEL
    for i in range(DEPTH):
        xn = rms_norm(h, g_mix[i])
        z = xn @ w_in[i]
        b, c, hc = z[..., :o1], z[..., o1:o2], z[..., o2:o3]
        v = z[..., o3:o4]
        gate_a = jax.nn.sigmoid(z[..., o4:o5])
        gate_b = jax.nn.sigmoid(z[..., o5:])
        ya, cb = gated_short_conv(b, c, hc, conv_bufs[i], w_conv[i])
        ya = ya @ w_out_conv[i]
        yb, pb = multiscale_pool(v, pool_bufs[i], pos0, w_pool[i], pool_scale[i])
        h = h + (gate_a * ya + gate_b * yb) @ w_o[i]
        new_conv.append(cb)
        new_pool.append(pb)
        hn = rms_norm(h, g_mlp[i])
        h = h + jnp.square(jax.nn.relu(hn @ w_up[i])) @ w_down[i]
        gp = jax.nn.sigmoid(rms_norm(h, g_ple[i]) @ w_ple_gate[i])
        h = h + gp * (p[i] @ w_ple_proj[i])
    return rms_norm(h, g_final), jnp.stack(new_conv), jnp.stack(new_pool)


def setup_inputs(seed: int = 0) -> dict:
    key = jax.random.key(seed)
    ks = jax.random.split(key, 24)
    f32 = jnp.float32

    def nrm(k, shape, scale):
        return jax.random.normal(k, shape, f32) * scale

    def gain(k, shape):
        return 1.0 + 0.05 * jax.random.normal(k, shape, f32)

    return {
        "x_prompt": nrm(ks[0], (BATCH, SEQ, D_MODEL), 1.0),
        "x_sample": nrm(ks[1], (DEC_BATCH, DEC_SEQ, D_MODEL), 1.0),
        "state_conv": nrm(ks[2], (DEPTH, DEC_BATCH, CONV_W - 1, CONV_WIDTH), 1.0),
        "state_pool": nrm(ks[3], (DEPTH, DEC_BATCH, POOL_BUF, POOL_WIDTH), 1.0),
        "p_prompt": nrm(ks[4], (DEPTH, BATCH, SEQ, PLE_DIM), 1.0),
        "p_sample": nrm(ks[5], (DEPTH, DEC_BATCH, DEC_SEQ, PLE_DIM), 1.0),
        "g_mix": gain(ks[6], (DEPTH, D_MODEL)),
        "w_in": nrm(ks[7], (DEPTH, D_MODEL, IN_COLS), D_MODEL ** -0.5),
        "w_conv": nrm(ks[8], (DEPTH, CONV_W, CONV_WIDTH), CONV_W ** -0.5),
        "w_out_conv": nrm(ks[9], (DEPTH, CONV_WIDTH, D_MODEL), CONV_WIDTH ** -0.5),
        "w_pool": nrm(ks[10], (DEPTH, N_POOL_GROUPS, POOL_GW, POOL_GW_OUT), POOL_GW ** -0.5),
        "pool_scale": gain(ks[11], (DEPTH, D_MODEL)),
        "w_o": nrm(ks[12], (DEPTH, D_MODEL, D_MODEL), D_MODEL ** -0.5),
        "g_mlp": gain(ks[13], (DEPTH, D_MODEL)),
        "w_up": nrm(ks[14], (DEPTH, D_MODEL, D_FF), D_MODEL ** -0.5),
        "w_down": nrm(ks[15], (DEPTH, D_FF, D_MODEL), D_FF ** -0.5),
        "g_ple": gain(ks[16], (DEPTH, D_MODEL)),
        "w_ple_gate": nrm(ks[17], (DEPTH, D_MODEL, D_MODEL), D_MODEL ** -0.5),
        "w_ple_proj": nrm(ks[18], (DEPTH, PLE_DIM, D_MODEL), PLE_DIM ** -0.5),
        "g_final": gain(ks[19], (D_MODEL,)),
    }


def reference(x_prompt, x_sample, state_conv, state_pool, p_prompt, p_sample, g_mix, w_in,
              w_conv, w_out_conv, w_pool, pool_scale, w_o, g_mlp, w_up, w_down, g_ple,
              w_ple_gate, w_ple_proj, g_final):
    zero_conv = jnp.zeros((DEPTH, x_prompt.shape[0], CONV_W - 1, CONV_WIDTH), x_prompt.dtype)
    zero_pool = jnp.zeros((DEPTH, x_prompt.shape[0], POOL_BUF, POOL_WIDTH), x_prompt.dtype)
    y_prompt, new_conv_prompt, new_pool_prompt = trunk(
        x_prompt, p_prompt, zero_conv, zero_pool, 0, g_mix, w_in, w_conv, w_out_conv,
        w_pool, pool_scale, w_o, g_mlp, w_up, w_down, g_ple, w_ple_gate, w_ple_proj, g_final)
    y_sample, new_conv_sample, new_pool_sample = trunk(
        x_sample, p_sample, state_conv, state_pool, PAST_LEN, g_mix, w_in, w_conv, w_out_conv,
        w_pool, pool_scale, w_o, g_mlp, w_up, w_down, g_ple, w_ple_gate, w_ple_proj, g_final)
    return (y_prompt, y_sample, new_conv_prompt, new_pool_prompt, new_conv_sample, new_pool_sample)
```

```python
import numpy as np
from contextlib import ExitStack

import concourse.bass as bass
import concourse.mybir as mybir
from concourse.bass_utils import run_bass_kernel_spmd

F32 = mybir.dt.float32
BF16 = mybir.dt.bfloat16
AF = mybir.ActivationFunctionType
ALU = mybir.AluOpType

P = 128
D = 1024
KC = 8
NCORE = 8
SEQ = 2048
NSEQ = 16
DSEQ = 8
NSAMP = NSEQ * DSEQ
PLE = 256
DFF = 4096
EPS = 1e-6
NTMAX = 768
WINDOWS = (2, 4, 8, 16)

STS = [
    (0, 768, False, [("p", 0, 384), ("p", 384, 768)]),
    (768, 768, False, [("p", 0, 384), ("p", 384, 768)]),
    (1536, 512, True, [("p", 0, 256), ("m", 256, 640)]),
]
TW = 384

CV_GMIX, CV_GMLP, CV_GPLE, CV_GFIN, CV_PSCALE, CV_WCONV = 0, 8, 16, 24, 32, 40


class Res:
    __slots__ = ("name", "last_w", "readers", "gen", "dsem", "dcount", "excl")

    def __init__(self, name):
        self.name = name
        self.excl = False
        self.last_w = None
        self.readers = {}
        self.gen = 0
        self.dsem = None
        self.dcount = 0


class H:
    __slots__ = ("res", "gen", "ap")

    def __init__(self, res, ap):
        self.res = res
        self.gen = res.gen
        self.ap = ap


def _res(x):
    if isinstance(x, H):
        assert x.res.gen == x.gen, f"stale ring handle {x.res.name}"
        return x.res
    return x


class Eng:
    def __init__(self, name, sem):
        self.name = name
        self.sem = sem
        self.count = 0
        self.waited = {}
        self.q = []


class Prog:
    def __init__(self, nc, stack):
        self.nc = nc
        self.stack = stack
        self.eng = {}
        for n in ("pe", "act", "dve", "pool", "sp"):
            s = stack.enter_context(nc.semaphore("sem_" + n))
            self.eng[n] = Eng(n, s)
        self.nsem = 0
        self.final_events = {}

    def new_sem(self, name):
        self.nsem += 1
        return self.stack.enter_context(self.nc.semaphore(f"ds{self.nsem}_{name}"))

    def _deps(self, reads, writes):
        deps = {}

        def add(ev):
            if ev is None:
                return
            s, v = ev
            k = id(s)
            if k not in deps or deps[k][1] < v:
                deps[k] = (s, v)

        for r in reads:
            r = _res(r)
            add(r.last_w)
            if r.excl:
                for ev in r.readers.values():
                    add(ev)
        for w in writes:
            w = _res(w)
            add(w.last_w)
            for ev in w.readers.values():
                add(ev)
        return deps

    def _emit_waits(self, e, deps, skip_self=False):
        for k, (s, v) in deps.items():
            if skip_self and s is e.sem:
                continue
            if e.waited.get(k, 0) >= v:
                continue
            e.waited[k] = v
            e.q.append(lambda eng, s=s, v=v: eng.wait_ge(s, v))

    def _record(self, ev, reads, writes):
        s, v = ev
        for r in reads:
            r = _res(r)
            k = id(s)
            if k not in r.readers or r.readers[k][1] < v:
                r.readers[k] = ev
        for w in writes:
            w = _res(w)
            w.last_w = ev
            w.readers = {}

    def op(self, en, fn, reads=(), writes=()):
        e = self.eng[en]
        deps = self._deps(reads, writes)
        self._emit_waits(e, deps, skip_self=(en == "pe"))
        e.count += 1
        ev = (e.sem, e.count)

        def run(eng, fn=fn, sem=e.sem):
            ins = fn(eng)
            ins.then_inc(sem, 1)

        e.q.append(run)
        self._record(ev, reads, writes)
        return ev

    def dma(self, en, fn, reads=(), writes=(), track=None, final=False, nodeps=False):
        e = self.eng[en]
        deps = self._deps(reads, writes)
        if not nodeps:
            self._emit_waits(e, deps)
        t = _res(track)
        if t.dsem is None:
            t.dsem = self.new_sem(t.name)
        t.dcount += 16
        ev = (t.dsem, t.dcount)

        def run(eng, fn=fn, sem=t.dsem):
            ins = fn(eng)
            ins.then_inc(sem, 16)

        e.q.append(run)
        self._record(ev, reads, writes)
        if final:
            self.final_events[id(t.dsem)] = ev
        return ev


class Ring:
    def __init__(self, name, aps):
        self.slots = [(Res(f"{name}{i}"), ap) for i, ap in enumerate(aps)]
        self.i = 0

    def alloc(self):
        r, ap = self.slots[self.i % len(self.slots)]
        self.i += 1
        r.gen += 1
        return H(r, ap)


def build_program():
    nc = bass.Bass("TRN2", target_bir_lowering=False)

    def din(name, shape):
        return nc.dram_tensor(name, shape, F32, kind="ExternalInput").ap()

    def dout(name, shape):
        return nc.dram_tensor(name, shape, F32, kind="ExternalOutput").ap()

    xp = din("xp", [SEQ, D]); xs = din("xs", [NSAMP, D])
    pp = din("pp", [SEQ, PLE]); ps_ = din("ps", [NSAMP, PLE])
    sc = din("sc", [NSEQ * 2, D]); sp = din("sp", [NSEQ * 15, D])
    g_mix = din("g_mix", [1, D]); g_mlp = din("g_mlp", [1, D]); g_ple = din("g_ple", [1, D])
    g_final = din("g_final", [1, D]); pool_scale = din("pool_scale", [1, D])
    w_conv = din("w_conv", [3, D])
    w_in = din("w_in", [D, 6 * D]); w_out_conv = din("w_out_conv", [D, D])
    w_pool = din("w_pool", [4, 256, 256]); w_o = din("w_o", [D, D])
    w_up = din("w_up", [D, DFF]); w_down = din("w_down", [DFF, D])
    w_ple_gate = din("w_ple_gate", [D, D]); w_ple_proj = din("w_ple_proj", [PLE, D])
    yp = dout("yp", [SEQ, D]); ys = dout("ys", [NSAMP, D])
    ncp = dout("ncp", [2, D]); npp = dout("npp", [15, D])
    ncs = dout("ncs", [NSEQ * 2, D]); nps = dout("nps", [NSEQ * 15, D])

    with ExitStack() as stack:
        def sb(name, shape, dt):
            return stack.enter_context(nc.sbuf_tensor(name, shape, dt))

        hT = sb("hT", [P, KC, NTMAX], F32)
        xn = sb("xn", [P, KC, NTMAX], BF16)
        ua = sb("ua", [P, KC, NTMAX], BF16)
        mg = sb("mg", [P, KC, NTMAX], BF16)
        pooled = sb("pooled", [P, KC, NTMAX], BF16)
        pT = sb("pT", [P, 2, NTMAX], BF16)
        NEXT = 2
        u_ext = [sb(f"u_ext{i}", [P, 2 + NTMAX], F32) for i in range(NEXT)]
        u_exs = [sb(f"u_exs{i}", [P, 10 * NSEQ], F32) for i in range(NEXT)]
        v_ext = [sb(f"v_ext{i}", [P, 15 + NTMAX], F32) for i in range(NEXT)]
        v_exs = [sb(f"v_exs{i}", [P, 23 * NSEQ], F32) for i in range(NEXT)]
        sAp = sb("sAp", [P, 15 + NTMAX], F32); sAs = sb("sAs", [P, 23 * NSEQ], F32)
        sBp = sb("sBp", [P, 15 + NTMAX], F32); sBs = sb("sBs", [P, 23 * NSEQ], F32)
        NW = 12
        wring_t = [sb(f"wr{i}", [P, 2048], BF16) for i in range(NW)]
        wpool_sb = sb("wpool_sb", [P, 4, 2, 256], BF16)
        pproj_sb = sb("pproj_sb", [P, 2, D], BF16)
        NTF = 9
        tmpF_t = [sb(f"tf{i}", [P, TW], F32) for i in range(NTF)]
        NTB = 11
        tmpB_t = [sb(f"tb{i}", [P, TW], BF16) for i in range(NTB)]
        sqbuf = sb("sqbuf", [P, KC, TW], BF16)
        yo_t = [sb(f"yo{i}", [P, 512], F32) for i in range(6)]
        stg_t = [sb(f"stg{i}", [P, TW], F32) for i in range(2)]
        ident = sb("ident", [P, P], F32)
        ones_b = sb("ones_b", [P, P], BF16)
        cstage = sb("cstage", [P, P], F32)
        cvec = sb("cvec", [P, 64], F32)
        inv_t = sb("inv_t", [P, 15], F32)
        invcnt = sb("invcnt", [P, 4, 15], F32)
        g_bc = sb("g_bc", [P, D], F32)
        rtok = sb("rtok", [P, 3, 4], F32)
        u_hist = sb("u_hist", [P, KC, 2], F32)
        v_hist = sb("v_hist", [P, KC, 15], F32)
        banks_t = [stack.enter_context(nc.psum_tensor(f"bank{i}", [P, 512], F32)) for i in range(8)]

        pg = Prog(nc, stack)
        op, dma = pg.op, pg.dma

        R_hT = [[Res(f"hT{j}_{t}") for t in range(3)] for j in range(KC)]
        R_xn = [[Res(f"xn{j}_{t}") for t in range(3)] for j in range(KC)]
        R_ua = [[Res(f"ua{j}_{t}") for t in range(3)] for j in range(KC)]
        R_mg = [[Res(f"mg{j}_{t}") for t in range(3)] for j in range(KC)]
        R_pl = [[Res(f"pl{j}_{t}") for t in range(3)] for j in range(KC)]
        R_pT = [Res(f"pT{t}") for t in range(3)]
        R_uext = [Res(f"uext{i}") for i in range(NEXT)]
        R_vext = [Res(f"vext{i}") for i in range(NEXT)]
        R_sA = Res("sA"); R_sB = Res("sB"); R_sAs = Res("sAs"); R_sBs = Res("sBs")
        R_sq = Res("sq")
        R_ident = Res("ident"); R_ones = Res("ones"); R_cvec = Res("cvec"); R_cstage = [Res(f"cstage{i}") for i in range(6)]
        R_invt = Res("inv_t"); R_invcnt = Res("invcnt")
        R_uh = [Res(f"uh{j}") for j in range(KC)]
        R_vh = [Res(f"vh{j}") for j in range(KC)]
        R_wpool = Res("wpool"); R_pproj = Res("pproj")
        R_gbc = Res("g_bc"); R_onesf = Res("ones_f"); R_rtok = [Res(f"rtok{t}") for t in range(3)]
        R_dd = Res("dram2dram")

        banks = Ring("bank", [b for b in banks_t])
        for r_, _ in banks.slots:
            r_.excl = True
        tmpF = Ring("tf", [t for t in tmpF_t])
        tmpB = Ring("tb", [t for t in tmpB_t])
        yor = Ring("yo", [t for t in yo_t])
        stgr = Ring("stg", [t for t in stg_t])

        class WStream:
            def __init__(self):
                self.res = [Res(f"wslot{i}") for i in range(NW)]
                self.specs = []
                self.loaded = 0
                self.done_upto = 0
                self.done = []
                self.first_reads = ()

            def add(self, spec):
                self.specs.append(spec)
                self.done.append(False)
                return len(self.specs) - 1

            def pump(self):
                while self.loaded < len(self.specs):
                    n = self.loaded
                    if n >= NW and not self.done[n - NW]:
                        break
                    src, kind = self.specs[n]
                    slot = wring_t[n % NW]
                    r = self.res[n % NW]
                    if kind == "k8":
                        dst = slot[:, :].rearrange("p (k c) -> p k c", k=8)
                        s = src.rearrange("(k p) c -> p k c", p=P)
                    else:
                        raise AssertionError(kind)
                    first_rd = self.first_reads if n == 0 else ()
                    dma("pool", lambda e, dst=dst, s=s: e.dma_start(out=dst, in_=s),
                        reads=first_rd, writes=(r,), track=r)
                    self.loaded += 1

            def view(self, n):
                assert n < self.loaded, "weight tile not yet loaded (ring too small)"
                assert n >= self.loaded - NW
                return wring_t[n % NW][:, :].rearrange("p (k c) -> p k c", k=8), self.res[n % NW]

            def finish(self, n):
                self.done[n] = True
                self.pump()

        ws = WStream()

        def wtile(src2d, c0):
            return ws.add((src2d[:, c0:c0 + 256], "k8"))

        pending = []

        def after_groups(k, key, fn):
            pending.append([k, key, fn])

        def tick():
            fire = [p for p in pending if p[0] <= 1]
            for p in pending:
                p[0] -= 1
            for p in fire:
                pending.remove(p)
            for p in fire:
                p[2]()

        def ensure(key):
            for p in list(pending):
                if p[1] == key:
                    pending.remove(p)
                    p[2]()

        def ensure_t(ti):
            for p in list(pending):
                if p in pending and p[1][1] == ti:
                    pending.remove(p)
                    p[2]()

        def flush_all():
            while pending:
                p = pending.pop(0)
                p[2]()

        def mm_group(bank, pairs, reads, n, do_tick=True):
            def fn(e, pairs=pairs, bank=bank, n=n):
                last = None
                L = len(pairs)
                for i, (l, r) in enumerate(pairs):
                    last = e.matmul(bank.ap[:, 0:n], lhsT=l, rhs=r, start=(i == 0), stop=(i == L - 1))
                return last
            ev = op("pe", fn, reads=reads, writes=(bank,))
            if do_tick:
                tick()
            return ev

        def pe_group(fn, reads, writes):
            ev = op("pe", fn, reads=reads, writes=writes)
            tick()
            return ev

        op("pool", lambda e: e.memset(ident[:, :], 0.0), writes=(R_ident,))
        op("pool", lambda e: e.affine_select(out=ident[:, :], in_=ident[:, :], pattern=[[-1, P]],
                                               compare_op=ALU.not_equal, fill=1.0, base=0,
                                               channel_multiplier=1),
           reads=(), writes=(R_ident,))
        op("dve", lambda e: e.memset(ones_b[:, :], 1.0), writes=(R_ones,))
        vecs = [(g_mix, CV_GMIX), (g_mlp, CV_GMLP), (g_ple, CV_GPLE), (g_final, CV_GFIN), (pool_scale, CV_PSCALE)]
        for i, (v, c) in enumerate(vecs):
            dma("sp", lambda e, v=v, c=c: e.dma_start(out=cstage[c:c + 8, :],
                                                    in_=v.rearrange("o (j p) -> (o j) p", p=P)),
                writes=(R_cstage[i],), track=R_cstage[i])
        dma("sp", lambda e: e.dma_start(out=cstage[CV_WCONV:CV_WCONV + 24, :],
                                        in_=w_conv.rearrange("k (j p) -> (k j) p", p=P)),
            writes=(R_cstage[5],), track=R_cstage[5])
        dma("pool", lambda e: e.dma_start(out=wpool_sb[:, :, :, :],
                                          in_=w_pool.rearrange("g (k p) d -> p g k d", p=P)),
            writes=(R_wpool,), track=R_wpool)
        dma("pool", lambda e: e.dma_start(out=pproj_sb[:, :, :],
                                          in_=w_ple_proj.rearrange("(k p) d -> p k d", p=P)),
            writes=(R_pproj,), track=R_pproj)
        bk = banks.alloc()
        op("pe", lambda e, bk=bk: e.transpose(bk.ap[:, 0:64], cstage[0:64, :], ident[0:64, 0:64]),
           reads=R_cstage + [R_ident], writes=(bk,))
        op("dve", lambda e, bk=bk: e.tensor_copy(out=cvec[:, :], in_=bk.ap[:, 0:64]),
           reads=(bk,), writes=(R_cvec,))
        op("dve", lambda e: e.memset(cstage[:, :], 1.0), writes=R_cstage + [R_onesf])
        for t in range(15):
            op("dve", lambda e, t=t: e.memset(inv_t[:, t:t + 1], 1.0 / (t + 1)), writes=(R_invt,))
        for g, w in enumerate(WINDOWS):
            op("dve", lambda e, g=g, w=w: e.tensor_scalar(out=invcnt[:, g, :], in0=inv_t[:, :], scalar1=1.0 / w,
                                                        scalar2=None, op0=ALU.max),
               reads=(R_invt,), writes=(R_invcnt,))
        def cv(col):
            return cvec[:, col:col + 1]

        grs = []
        for half in range(2):
            gr = yor.alloc()
            dma("sp", lambda e, gr=gr, half=half: e.dma_start(out=gr.ap[0:1, :], in_=g_final[0:1, half * 512:(half + 1) * 512]),
                writes=(gr,), track=gr)
            grs.append(gr)

        def setup_gbc():
          for half in range(2):
              gr = grs[half]
              bk = banks.alloc()
              op("pe", lambda e, bk=bk, gr=gr: e.matmul(bk.ap[:, :], lhsT=cstage[0:1, :], rhs=gr.ap[0:1, :], start=True, stop=True),
                 reads=(gr, R_onesf), writes=(bk,))
              op("dve", lambda e, bk=bk, half=half: e.tensor_copy(out=g_bc[:, half * 512:(half + 1) * 512], in_=bk.ap[:, :]),
                 reads=(bk,), writes=(R_gbc,))

        def norm_A(st, ti):
            _, c0, c1 = STS[st][3][ti]
            n = c1 - c0
            for j in range(KC):
                op("act", lambda e, j=j: e.activation(out=sqbuf[:, j, 0:n], in_=hT[:, j, c0:c1], func=AF.Square),
                   reads=(R_hT[j][ti],), writes=(R_sq,))

        def norm_BC(st, ti, gcol, key, inplace_f32=False, trickle=False):
            _, c0, c1 = STS[st][3][ti]
            n = c1 - c0
            bk = banks.alloc()
            mm_group(bk, [(ones_b[:, :], sqbuf[:, j, 0:n]) for j in range(KC)], reads=(R_sq, R_ones), n=n, do_tick=False)
            s1 = tmpF.alloc()
            op("act", lambda e: e.activation(out=s1.ap[:, 0:n], in_=bk.ap[:, 0:n], func=AF.Sqrt,
                                             bias=EPS, scale=1.0 / D),
               reads=(bk,), writes=(s1,))
            rb = tmpF.alloc() if trickle else bk
            op("dve", lambda e: e.reciprocal(out=rb.ap[:, 0:n], in_=s1.ap[:, 0:n]), reads=(s1,), writes=(rb,))

            def apply(j):
                assert rb.res.gen == rb.gen, "norm scale bank recycled before use"
                op("dve", lambda e, j=j: e.scalar_tensor_tensor(out=xn[:, j, c0:c1], in0=hT[:, j, c0:c1],
                                                              scalar=cv(gcol + j), in1=rb.ap[:, 0:n],
                                                              op0=ALU.mult, op1=ALU.mult),
                   reads=(R_hT[j][ti], rb, R_cvec), writes=(R_xn[j][ti],))
            step = trickle if trickle else 0
            for j in range(KC):
                if trickle:
                    after_groups(step * (j + 1), key, lambda j=j: apply(j))
                else:
                    apply(j)

        def final_norm_B(st, ti):
            _, c0, c1 = STS[st][3][ti]
            nb = (c1 - c0) // P
            bk = banks.alloc()

            def fn(e):
                last = None
                for b in range(nb):
                    for j in range(KC):
                        last = e.matmul(bk.ap[:, b:b + 1], lhsT=sqbuf[:, j, b * P:(b + 1) * P], rhs=ones_b[:, 0:1],
                                        start=(j == 0), stop=(j == KC - 1))
                return last
            op("pe", fn, reads=(R_sq, R_ones), writes=(bk,))
            s1 = tmpF.alloc()
            op("act", lambda e: e.activation(out=s1.ap[:, 0:nb], in_=bk.ap[:, 0:nb], func=AF.Sqrt, bias=EPS, scale=1.0 / D),
               reads=(bk,), writes=(s1,))
            op("dve", lambda e: e.reciprocal(out=rtok[:, ti, 0:nb], in_=s1.ap[:, 0:nb]), reads=(s1,), writes=(R_rtok[ti],))

        def final_norm_deferred(st, ti, key, k=13):
            norm_A(st, ti)
            after_groups(k, key, lambda: final_norm_B(st, ti))

        def norm_deferred(st, ti, gcol, key, inplace_f32=False, k=3, trickle=False):
            norm_A(st, ti)
            after_groups(k, key, lambda: norm_BC(st, ti, gcol, key, inplace_f32, trickle))

        ua_f = ua[:, :, :].rearrange("p k n -> p (k n)").bitcast(F32)
        mg_f = mg[:, :, :].rearrange("p k n -> p (k n)").bitcast(F32)
        pl_f = pooled[:, :, :].rearrange("p k n -> p (k n)").bitcast(F32)
        CHB = NTMAX * 2

        def cover(R, lo, hi):
            return [R[k][t] for k in range(lo // CHB, (hi - 1) // CHB + 1) for t in range(3)]

        def xstage(bi):
            buf, R = (ua_f, R_ua) if bi < 3 else (mg_f, R_mg)
            i = bi % 3
            return buf[:, i * D:(i + 1) * D], cover(R, i * D * 4, (i + 1) * D * 4)

        def pstage(bi):
            return pl_f[:, bi * PLE:(bi + 1) * PLE], cover(R_pl, bi * PLE * 4, (bi + 1) * PLE * 4)

        def prefetch_inputs(st, which, nodeps=False):
            p0_, NP_, _, tiles_ = STS[st]
            nblk = tiles_[-1][2] // P
            for bi in range(nblk):
                b0 = bi * P
                if b0 < NP_:
                    xsrc, psrc = xp[p0_ + b0:p0_ + b0 + P, :], pp[p0_ + b0:p0_ + b0 + P, :]
                else:
                    assert b0 == NP_
                    xsrc = xs.rearrange("(s t) d -> t s d", t=DSEQ)
                    psrc = ps_.rearrange("(s t) d -> t s d", t=DSEQ)
                if (which == "ua" and bi < 3) or (which == "mg" and bi >= 3):
                    dst, rs = xstage(bi)
                    dma("sp", lambda e, dst=dst, xsrc=xsrc: e.dma_start(out=dst, in_=xsrc), writes=rs, track=rs[0],
                        nodeps=nodeps)
                if which == "ua":
                    dst, rs = pstage(bi)
                    dma("sp", lambda e, dst=dst, psrc=psrc: e.dma_start(out=dst, in_=psrc), writes=rs, track=rs[0],
                        nodeps=nodeps)

        prefetch_inputs(0, "ua", nodeps=True)
        prefetch_inputs(0, "mg", nodeps=True)
        ws.first_reads = (R_ua[0][0],)

        def state_in_dma(j):
            stg = stgr.alloc()
            dma("sp", lambda e, stg=stg, j=j: e.dma_start(out=stg.ap[0:120, 0:P], in_=sp[0:120, j * P:(j + 1) * P]),
                writes=(stg,), track=stg)
            dma("sp", lambda e, stg=stg, j=j: e.dma_start(out=stg.ap[0:120, P:2 * P], in_=sp[120:240, j * P:(j + 1) * P]),
                writes=(stg,), track=stg, nodeps=True)
            dma("sp", lambda e, stg=stg, j=j: e.dma_start(out=stg.ap[0:32, 2 * P:3 * P], in_=sc[0:32, j * P:(j + 1) * P]),
                writes=(stg,), track=stg, nodeps=True)
            return stg

        stash = []
        for st, (p0, NP, has_s, tiles) in enumerate(STS):
            NT = tiles[-1][2]
            ntile = len(tiles)
            ptiles = [ti for ti in range(ntile) if tiles[ti][0] in ("p", "m")]
            stiles = [ti for ti in range(ntile) if tiles[ti][0] in ("s", "m")]
            W1a = []
            for jp in range(4):
                W1a.append([wtile(w_in, off + jp * 256) for off in (0, 1024, 2048, 3072)])
            W1b = []
            for jp in range(4):
                W1b.append([wtile(w_out_conv, jp * 256), wtile(w_in, 4096 + jp * 256), wtile(w_in, 5120 + jp * 256)])
            W2 = [wtile(w_o, ip * 256) for ip in range(4)]
            W4 = []
            for q in range(4):
                ups = [wtile(w_up, q * 1024 + fp * 256) for fp in range(4)]
                dns = [wtile(w_down[q * 1024:(q + 1) * 1024, :], jp * 256) for jp in range(4)]
                W4.append((ups, dns))
            W5 = [wtile(w_ple_gate, jp * 256) for jp in range(4)]
            ws.pump()
            stgs = {}
            if has_s:
                stgs[0] = state_in_dma(0)

            for ti, (kind, c0, c1) in enumerate(tiles):
                n = c1 - c0
                for b0 in range(c0, c1, P):
                    xin_ap, xin_rs = xstage(b0 // P)
                    pin_ap, pin_rs = pstage(b0 // P)
                    for half in range(2):
                        bk = banks.alloc()

                        def fn(e, bk=bk, xin_ap=xin_ap, half=half):
                            last = None
                            for jj in range(4):
                                j = half * 4 + jj
                                last = e.transpose(bk.ap[:, jj * P:(jj + 1) * P], xin_ap[:, j * P:(j + 1) * P], ident[:, :])
                            return last
                        pe_group(fn, reads=xin_rs + [R_ident], writes=(bk,))
                        bv = bk.ap[:, :].rearrange("p (j t) -> p j t", j=4)
                        op("act", lambda e, bv=bv, half=half, b0=b0: e.activation(
                            out=hT[:, half * 4:half * 4 + 4, b0:b0 + P], in_=bv, func=AF.Copy),
                           reads=(bk,), writes=[R_hT[half * 4 + jj][ti] for jj in range(4)])
                        op("act", lambda e, bv=bv, half=half, b0=b0, c0=c0: e.activation(
                            out=sqbuf[:, half * 4:half * 4 + 4, b0 - c0:b0 - c0 + P], in_=bv, func=AF.Square),
                           reads=(bk,), writes=(R_sq,))
                    bk = banks.alloc()

                    def fnp(e, bk=bk, pin_ap=pin_ap):
                        last = None
                        for k in range(2):
                            last = e.transpose(bk.ap[:, k * P:(k + 1) * P], pin_ap[:, k * P:(k + 1) * P], ident[:, :])
                        return last
                    pe_group(fnp, reads=pin_rs + [R_ident], writes=(bk,))
                    op("dve", lambda e, bk=bk, b0=b0: e.tensor_copy(
                        out=pT[:, :, b0:b0 + P], in_=bk.ap[:, 0:2 * P].rearrange("p (k t) -> p k t", k=2)),
                       reads=(bk,), writes=(R_pT[ti],))
                flush_all()
                norm_BC(st, ti, CV_GMIX, ("n1", ti))
                if ti == 0 and stash:
                    stash.pop()()

            def pre_chunk(j, stg):
                xi = j % NEXT
                ue, us_, ve, vs_ = u_ext[xi], u_exs[xi], v_ext[xi], v_exs[xi]
                Ru, Rv = R_uext[xi], R_vext[xi]
                if st == 0:
                    op("dve", lambda e, ue=ue: e.memset(ue[:, 0:2], 0.0), writes=(Ru,))
                    op("dve", lambda e, ve=ve: e.memset(ve[:, 0:15], 0.0), writes=(Rv,))
                else:
                    op("dve", lambda e, ue=ue, j=j: e.tensor_copy(out=ue[:, 0:2], in_=u_hist[:, j, :]),
                       reads=(R_uh[j],), writes=(Ru,))
                    op("dve", lambda e, ve=ve, j=j: e.tensor_copy(out=ve[:, 0:15], in_=v_hist[:, j, :]),
                       reads=(R_vh[j],), writes=(Rv,))
                if has_s:
                    bk = banks.alloc()

                    def fns(e, bk=bk, stg=stg):
                        e.transpose(bk.ap[:, 0:120], stg.ap[0:120, 0:P], ident[0:120, 0:120])
                        e.transpose(bk.ap[:, 128:248], stg.ap[0:120, P:2 * P], ident[0:120, 0:120])
                        return e.transpose(bk.ap[:, 256:288], stg.ap[0:32, 2 * P:3 * P], ident[0:32, 0:32])
                    pe_group(fns, reads=(stg, R_ident), writes=(bk,))
                    vsv = vs_[:, 0:15 * NSEQ].rearrange("p (h s) -> p s h", s=NSEQ)
                    usv = us_[:, 0:2 * NSEQ].rearrange("p (h s) -> p s h", s=NSEQ)
                    op("act", lambda e, bk=bk, vsv=vsv: e.activation(
                        out=vsv[:, 0:8, :], in_=bk.ap[:, 0:120].rearrange("p (s h) -> p s h", h=15), func=AF.Copy),
                       reads=(bk,), writes=(Rv,))
                    op("act", lambda e, bk=bk, vsv=vsv: e.activation(
                        out=vsv[:, 8:16, :], in_=bk.ap[:, 128:248].rearrange("p (s h) -> p s h", h=15), func=AF.Copy),
                       reads=(bk,), writes=(Rv,))
                    op("act", lambda e, bk=bk, usv=usv: e.activation(
                        out=usv, in_=bk.ap[:, 256:288].rearrange("p (s h) -> p s h", h=2), func=AF.Copy),
                       reads=(bk,), writes=(Ru,))

            def mm_chunk(j, wv):
                jc = (j % 2) * P
                xi = j % NEXT
                ue, us_, ve, vs_ = u_ext[xi], u_exs[xi], v_ext[xi], v_exs[xi]
                Ru, Rv = R_uext[xi], R_vext[xi]
                for ti, (kind, c0, c1) in enumerate(tiles):
                    n = c1 - c0
                    ensure_t(ti)
                    xr = [R_xn[k][ti] for k in range(KC)]
                    bks = []
                    for wi in range(4):
                        wap, wr = wv[wi]
                        bk = banks.alloc()
                        mm_group(bk, [(wap[:, k, jc:jc + P], xn[:, k, c0:c1]) for k in range(KC)],
                                 reads=xr + [wr], n=n)
                        bks.append(bk)
                    Bb, Bc, Bh, Bv = bks
                    if kind == "p":
                        sgs = [("p", c0, c1)]
                    elif kind == "s":
                        sgs = [("s", c0, c1)]
                    else:
                        sgs = [("p", c0, NP), ("s", NP, c1)]
                    csb = tmpF.alloc()
                    op("act", lambda e, csb=csb, Bc=Bc, n=n: e.activation(out=csb.ap[:, 0:n], in_=Bc.ap[:, 0:n], func=AF.Copy),
                       reads=(Bc,), writes=(csb,))
                    y3 = tmpF.alloc()
                    for (sk, a0, a1) in sgs:
                        r0, r1 = a0 - c0, a1 - c0
                        if sk == "p":
                            def uo(s, ue=ue, a0=a0, a1=a1):
                                return ue[:, 2 + a0 - s:2 + a1 - s]
                            vdst = ve[:, 15 + a0:15 + a1]

                            def vw(a):
                                return a
                        else:
                            def uo(s, us_=us_):
                                return us_[:, (2 - s) * NSEQ:(10 - s) * NSEQ]
                            vdst = vs_[:, 15 * NSEQ:23 * NSEQ]

                            def vw(a):
                                return a
                        op("dve", lambda e, csb=csb, Bh=Bh, r0=r0, r1=r1, uo=uo, vw=vw: e.tensor_tensor(
                            out=uo(0), in0=vw(csb.ap[:, r0:r1]), in1=vw(Bh.ap[:, r0:r1]), op=ALU.mult),
                           reads=(csb, Bh), writes=(Ru,))
                        op("act", lambda e, Bc=Bc, r0=r0, r1=r1, uo=uo, vw=vw, j=j: e.activation(
                            out=vw(Bc.ap[:, r0:r1]), in_=uo(0), func=AF.Copy, scale=cv(CV_WCONV + 16 + j)),
                           reads=(Ru, R_cvec), writes=(Bc,))
                        op("dve", lambda e, Bc=Bc, Bh=Bh, r0=r0, r1=r1, uo=uo, vw=vw, j=j: e.scalar_tensor_tensor(
                            out=vw(Bh.ap[:, r0:r1]), in0=uo(1), scalar=cv(CV_WCONV + 8 + j), in1=vw(Bc.ap[:, r0:r1]),
                            op0=ALU.mult, op1=ALU.add),
                           reads=(Ru, Bc, R_cvec), writes=(Bh,))
                        op("dve", lambda e, Bh=Bh, y3=y3, r0=r0, r1=r1, uo=uo, vw=vw, j=j: e.scalar_tensor_tensor(
                            out=vw(y3.ap[:, r0:r1]), in0=uo(2), scalar=cv(CV_WCONV + j), in1=vw(Bh.ap[:, r0:r1]),
                            op0=ALU.mult, op1=ALU.add),
                           reads=(Ru, Bh, R_cvec), writes=(y3,))
                        op("act", lambda e, Bv=Bv, r0=r0, r1=r1, vdst=vdst, vw=vw: e.activation(
                            out=vdst, in_=vw(Bv.ap[:, r0:r1]), func=AF.Copy),
                           reads=(Bv,), writes=(Rv,))
                    op("dve", lambda e, y3=y3, Bb=Bb, n=n, j=j, c0=c0, c1=c1: e.tensor_tensor(
                        out=ua[:, j, c0:c1], in0=Bb.ap[:, 0:n], in1=y3.ap[:, 0:n], op=ALU.mult),
                       reads=(Bb, y3), writes=(R_ua[j][ti],))

            def pool_chunk(j):
                xi = j % NEXT
                ue, us_, ve, vs_ = u_ext[xi], u_exs[xi], v_ext[xi], v_exs[xi]
                Ru, Rv = R_uext[xi], R_vext[xi]
                g = j // 2
                w = WINDOWS[g]
                segs = [("p", ve, sAp, sBp, 15 + NP, R_sA, R_sB)]
                if has_s:
                    segs.append(("s", vs_, sAs, sBs, 23, R_sAs, R_sBs))
                finals = []
                for (kind, vb, A, B, L, RA, RB) in segs:
                    if kind == "p":
                        def sl(buf, a, b):
                            return buf[:, a:b]
                    else:
                        def sl(buf, a, b):
                            return buf[:, a * NSEQ:b * NSEQ]
                    op("pool", lambda e, sl=sl, vb=vb, A=A, L=L: e.tensor_tensor(
                        out=sl(A, 1, L), in0=sl(vb, 1, L), in1=sl(vb, 0, L - 1), op=ALU.add),
                       reads=(Rv,), writes=(RA,))
                    S, RS = A, RA
                    if w >= 4:
                        op("pool", lambda e, sl=sl, A=A, B=B, L=L: e.tensor_tensor(
                            out=sl(B, 3, L), in0=sl(A, 3, L), in1=sl(A, 1, L - 2), op=ALU.add),
                           reads=(RA,), writes=(RB,))
                        S, RS = B, RB
                    if w >= 8:
                        op("pool", lambda e, sl=sl, A=A, B=B, L=L: e.tensor_tensor(
                            out=sl(A, 7, L), in0=sl(B, 7, L), in1=sl(B, 3, L - 4), op=ALU.add),
                           reads=(RB,), writes=(RA,))
                        S, RS = A, RA
                    if w >= 16:
                        op("pool", lambda e, sl=sl, A=A, B=B, L=L: e.tensor_tensor(
                            out=sl(B, 15, L), in0=sl(A, 15, L), in1=sl(A, 7, L - 8), op=ALU.add),
                           reads=(RA,), writes=(RB,))
                        S, RS = B, RB
                    if kind == "p":
                        pdst = pooled[:, j, 0:NP]
                        wres = [R_pl[j][ti] for ti in ptiles]
                    else:
                        pdst = pooled[:, j, NP:NP + NSAMP]
                        wres = [R_pl[j][ti] for ti in stiles]
                    def fin(sl=sl, S=S, RS=RS, vb=vb, L=L, pdst=pdst, wres=wres, kind=kind):
                        op("dve", lambda e: e.scalar_tensor_tensor(
                            out=pdst, in0=sl(S, 15, L), scalar=1.0 / w, in1=sl(vb, 15, L),
                            op0=ALU.mult, op1=ALU.subtract),
                           reads=(RS, Rv), writes=wres)
                        if kind == "p" and st == 0:
                            fx = tmpF.alloc()
                            op("dve", lambda e, fx=fx: e.tensor_tensor(
                                out=fx.ap[:, 0:15], in0=S[:, 15:30], in1=invcnt[:, g, :], op=ALU.mult),
                               reads=(RS, R_invcnt), writes=(fx,))
                            op("dve", lambda e, fx=fx: e.tensor_tensor(
                                out=pooled[:, j, 0:15], in0=fx.ap[:, 0:15], in1=vb[:, 15:30], op=ALU.subtract),
                               reads=(fx, Rv), writes=(R_pl[j][0],))
                    finals.append(fin)

                def final():
                    for f in finals:
                        f()
                    if st < len(STS) - 1:
                        op("dve", lambda e: e.tensor_copy(out=u_hist[:, j, :], in_=ue[:, NPc:NPc + 2]),
                           reads=(Ru,), writes=(R_uh[j],))
                        op("dve", lambda e: e.tensor_copy(out=v_hist[:, j, :], in_=ve[:, NPc:NPc + 15]),
                           reads=(Rv,), writes=(R_vh[j],))
                NPc = NP
                return final

            def state_out(j):
                xi = j % NEXT
                ue, us_, ve, vs_ = u_ext[xi], u_exs[xi], v_ext[xi], v_exs[xi]
                Ru, Rv = R_uext[xi], R_vext[xi]
                bkA = banks.alloc()

                def fno(e, bkA=bkA, vs_=vs_, us_=us_):
                    e.transpose(bkA.ap[0:128, 0:P], vs_[:, 8 * NSEQ:16 * NSEQ], ident[:, :])
                    e.transpose(bkA.ap[0:112, P:2 * P], vs_[:, 16 * NSEQ:23 * NSEQ], ident[:, :])
                    return e.transpose(bkA.ap[0:32, 2 * P:3 * P], us_[:, 8 * NSEQ:10 * NSEQ], ident[:, :])
                pe_group(fno, reads=(Rv, Ru, R_ident), writes=(bkA,))
                bkB = banks.alloc()

                def fno2(e, bkB=bkB, ve=ve, ue=ue, NP=NP):
                    e.transpose(bkB.ap[0:15, 0:P], ve[:, NP:NP + 15], ident[:, :])
                    return e.transpose(bkB.ap[0:2, P:2 * P], ue[:, NP:NP + 2], ident[:, :])
                pe_group(fno2, reads=(Rv, Ru, R_ident), writes=(bkB,))
                so = tmpF.alloc()
                op("act", lambda e, so=so, bkA=bkA: e.activation(out=so.ap[:, 0:3 * P], in_=bkA.ap[:, 0:3 * P], func=AF.Copy),
                   reads=(bkA,), writes=(so,))
                so2 = tmpF.alloc()
                op("act", lambda e, so2=so2, bkB=bkB: e.activation(out=so2.ap[0:15, 0:2 * P], in_=bkB.ap[0:15, 0:2 * P], func=AF.Copy),
                   reads=(bkB,), writes=(so2,))
                jcs = slice(j * P, (j + 1) * P)
                npv = nps.rearrange("(s r) d -> r s d", r=15)
                ncv = ncs.rearrange("(s t) d -> t s d", t=2)
                dma("sp", lambda e, so=so, jcs=jcs, npv=npv: e.dma_start(out=npv[0:8, :, jcs], in_=so.ap[0:128, 0:P]),
                    reads=(so,), track=so, final=True)
                dma("sp", lambda e, so=so, jcs=jcs, npv=npv: e.dma_start(out=npv[8:15, :, jcs], in_=so.ap[0:112, P:2 * P]),
                    reads=(so,), track=so, final=True)
                dma("sp", lambda e, so=so, jcs=jcs, ncv=ncv: e.dma_start(out=ncv[:, :, jcs], in_=so.ap[0:32, 2 * P:3 * P]),
                    reads=(so,), track=so, final=True)
                dma("sp", lambda e, so2=so2, jcs=jcs: e.dma_start(out=npp[0:15, jcs], in_=so2.ap[0:15, 0:P]),
                    reads=(so2,), track=so2, final=True)
                dma("sp", lambda e, so2=so2, jcs=jcs: e.dma_start(out=ncp[0:2, jcs], in_=so2.ap[0:2, P:2 * P]),
                    reads=(so2,), track=so2, final=True)

            pre_chunk(0, stgs.get(0))
            pending_final = None
            for j in range(KC):
                jp = j // 2
                if j % 2 == 0:
                    wv = [ws.view(n) for n in W1a[jp]]
                if has_s and j + 1 < KC:
                    stgs[j + 1] = state_in_dma(j + 1)
                mm_chunk(j, wv)
                if pending_final is not None:
                    pending_final()
                pending_final = pool_chunk(j)
                if has_s and j >= 1:
                    state_out(j - 1)
                if j + 1 < KC:
                    pre_chunk(j + 1, stgs.get(j + 1))
                if j % 2 == 1:
                    for n_ in W1a[jp]:
                        ws.finish(n_)
            pending_final()
            if has_s:
                state_out(KC - 1)

            for jp in range(4):
                (woc, rwoc), (wga, rga), (wgb, rgb) = [ws.view(n) for n in W1b[jp]]
                for j in (2 * jp, 2 * jp + 1):
                    jc = (j % 2) * P
                    g = j // 2
                    for ti, (kind, c0, c1) in enumerate(tiles):
                        n = c1 - c0
                        xr = [R_xn[k][ti] for k in range(KC)]
                        Bya = banks.alloc()
                        mm_group(Bya, [(woc[:, k, jc:jc + P], ua[:, k, c0:c1]) for k in range(KC)],
                                 reads=[R_ua[k][ti] for k in range(KC)] + [rwoc], n=n)
                        Bga = banks.alloc()
                        mm_group(Bga, [(wga[:, k, jc:jc + P], xn[:, k, c0:c1]) for k in range(KC)], reads=xr + [rga], n=n)
                        Byb = banks.alloc()
                        mm_group(Byb, [(wpool_sb[:, g, k, jc:jc + P], pooled[:, 2 * g + k, c0:c1]) for k in range(2)],
                                 reads=[R_pl[2 * g][ti], R_pl[2 * g + 1][ti], R_wpool], n=n)
                        Bgb = banks.alloc()
                        mm_group(Bgb, [(wgb[:, k, jc:jc + P], xn[:, k, c0:c1]) for k in range(KC)], reads=xr + [rgb], n=n)
                        ga = tmpF.alloc()
                        op("act", lambda e, ga=ga, Bga=Bga, n=n: e.activation(out=ga.ap[:, 0:n], in_=Bga.ap[:, 0:n], func=AF.Sigmoid),
                           reads=(Bga,), writes=(ga,))
                        gb = tmpF.alloc()
                        op("act", lambda e, gb=gb, Bgb=Bgb, n=n: e.activation(out=gb.ap[:, 0:n], in_=Bgb.ap[:, 0:n], func=AF.Sigmoid),
                           reads=(Bgb,), writes=(gb,))
                        t1 = tmpF.alloc()
                        op("dve", lambda e, t1=t1, ga=ga, Bya=Bya, n=n: e.tensor_tensor(
                            out=t1.ap[:, 0:n], in0=Bya.ap[:, 0:n], in1=ga.ap[:, 0:n], op=ALU.mult),
                           reads=(Bya, ga), writes=(t1,))
                        op("dve", lambda e, gb=gb, Byb=Byb, n=n, j=j: e.scalar_tensor_tensor(
                            out=Byb.ap[:, 0:n], in0=Byb.ap[:, 0:n], scalar=cv(CV_PSCALE + j), in1=gb.ap[:, 0:n],
                            op0=ALU.mult, op1=ALU.mult),
                           reads=(gb, R_cvec), writes=(Byb,))
                        op("dve", lambda e, t1=t1, Byb=Byb, n=n, j=j, c0=c0, c1=c1: e.tensor_tensor(
                            out=mg[:, j, c0:c1], in0=t1.ap[:, 0:n], in1=Byb.ap[:, 0:n], op=ALU.add),
                           reads=(t1, Byb), writes=(R_mg[j][ti],))
                for n_ in W1b[jp]:
                    ws.finish(n_)

            if st + 1 < len(STS):
                prefetch_inputs(st + 1, "ua")
            wov = [ws.view(n) for n in W2]
            for ti, (kind, c0, c1) in enumerate(tiles):
                n = c1 - c0
                for i in range(KC):
                    wo, rwo = wov[i // 2]
                    ic = (i % 2) * P
                    bk = banks.alloc()
                    mm_group(bk, [(wo[:, k, ic:ic + P], mg[:, k, c0:c1]) for k in range(KC)],
                             reads=[R_mg[k][ti] for k in range(KC)] + [rwo], n=n)
                    op("dve", lambda e, bk=bk, n=n, i=i, c0=c0, c1=c1: e.tensor_tensor(
                        out=hT[:, i, c0:c1], in0=bk.ap[:, 0:n], in1=hT[:, i, c0:c1], op=ALU.add),
                       reads=(bk,), writes=(R_hT[i][ti],))
                norm_deferred(st, ti, CV_GMLP, ("n2", ti))
            for n_ in W2:
                ws.finish(n_)
            if st + 1 < len(STS):
                prefetch_inputs(st + 1, "mg")

            units = [(q, ti) for q in range(4) for ti in range(ntile)]
            LA = 3
            acts = {}
            n_up = [0]

            def emit_up(idx):
                u, f = divmod(idx, 8)
                q, ti = units[u]
                _, c0, c1 = tiles[ti]
                n = c1 - c0
                ensure_t(ti)
                wu, rwu = ws.view(W4[q][0][f // 2])
                fc = (f % 2) * P
                bk = banks.alloc()
                mm_group(bk, [(wu[:, k, fc:fc + P], xn[:, k, c0:c1]) for k in range(KC)],
                         reads=[R_xn[k][ti] for k in range(KC)] + [rwu], n=n)
                rl = tmpF.alloc()
                op("act", lambda e, rl=rl, bk=bk, n=n: e.activation(out=rl.ap[:, 0:n], in_=bk.ap[:, 0:n], func=AF.Relu),
                   reads=(bk,), writes=(rl,))
                a = tmpB.alloc()
                op("act", lambda e, a=a, rl=rl, n=n: e.activation(out=a.ap[:, 0:n], in_=rl.ap[:, 0:n], func=AF.Square),
                   reads=(rl,), writes=(a,))
                acts.setdefault(u, []).append(a)

            for u, (q, ti) in enumerate(units):
                _, c0, c1 = tiles[ti]
                n = c1 - c0
                while n_up[0] < min((u + 1) * 8 + LA, len(units) * 8):
                    emit_up(n_up[0])
                    n_up[0] += 1
                for j in range(KC):
                    wd, rwd = ws.view(W4[q][1][j // 2])
                    jc = (j % 2) * P
                    bk = banks.alloc()
                    mm_group(bk, [(wd[:, f, jc:jc + P], acts[u][f].ap[:, 0:n]) for f in range(8)],
                             reads=acts[u] + [rwd], n=n)
                    op("dve", lambda e, bk=bk, n=n, j=j, c0=c0, c1=c1: e.tensor_tensor(
                        out=hT[:, j, c0:c1], in0=bk.ap[:, 0:n], in1=hT[:, j, c0:c1], op=ALU.add),
                       reads=(bk,), writes=(R_hT[j][ti],))
                if q == 3:
                    norm_deferred(st, ti, CV_GPLE, ("n3", ti), trickle=1)
                if ti == ntile - 1:
                    for n_ in W4[q][0] + W4[q][1]:
                        ws.finish(n_)

            wgv = [ws.view(n) for n in W5]
            for ti, (kind, c0, c1) in enumerate(tiles):
                n = c1 - c0
                ensure_t(ti)
                for j in range(KC):
                    wg, rwg = wgv[j // 2]
                    jc = (j % 2) * P
                    Bg = banks.alloc()
                    mm_group(Bg, [(wg[:, k, jc:jc + P], xn[:, k, c0:c1]) for k in range(KC)],
                             reads=[R_xn[k][ti] for k in range(KC)] + [rwg], n=n)
                    Bp = banks.alloc()
                    mm_group(Bp, [(pproj_sb[:, k, j * P:(j + 1) * P], pT[:, k, c0:c1]) for k in range(2)],
                             reads=[R_pT[ti], R_pproj], n=n)
                    gp = tmpF.alloc()
                    op("act", lambda e, gp=gp, Bg=Bg, n=n: e.activation(out=gp.ap[:, 0:n], in_=Bg.ap[:, 0:n], func=AF.Sigmoid),
                       reads=(Bg,), writes=(gp,))
                    tt = tmpF.alloc()
                    op("dve", lambda e, tt=tt, gp=gp, Bp=Bp, n=n: e.tensor_tensor(
                        out=tt.ap[:, 0:n], in0=Bp.ap[:, 0:n], in1=gp.ap[:, 0:n], op=ALU.mult),
                       reads=(Bp, gp), writes=(tt,))
                    op("pool", lambda e, tt=tt, n=n, j=j, c0=c0, c1=c1: e.tensor_tensor(
                        out=hT[:, j, c0:c1], in0=hT[:, j, c0:c1], in1=tt.ap[:, 0:n], op=ALU.add),
                       reads=(tt,), writes=(R_hT[j][ti],))
                final_norm_deferred(st, ti, ("n4", ti))
            for n_ in W5:
                ws.finish(n_)

            def store_tile(st, ti, p0=p0, tiles=tiles, NPs=NP):
                kind, c0, c1 = tiles[ti]
                ensure_t(ti)
                for b0 in range(c0, c1, P):
                    if b0 < NPs:
                        ydst = yp[p0 + b0:p0 + b0 + P, :]
                        ysl = lambda yd, half: yd[:, half * 512:(half + 1) * 512]
                    else:
                        ydst = ys.rearrange("(s t) d -> t s d", t=DSEQ)
                        ysl = lambda yd, half: yd[:, :, half * 512:(half + 1) * 512]
                    for half in range(2):
                        bk = banks.alloc()

                        def fny(e, bk=bk, half=half, b0=b0):
                            last = None
                            for jj in range(4):
                                j = half * 4 + jj
                                last = e.transpose(bk.ap[:, jj * P:(jj + 1) * P], hT[:, j, b0:b0 + P], ident[:, :])
                            return last
                        pe_group(fny, reads=[R_hT[half * 4 + jj][ti] for jj in range(4)] + [R_ident], writes=(bk,))
                        yo = yor.alloc()
                        bi = (b0 - c0) // P
                        op("dve", lambda e, bk=bk, yo=yo, half=half, bi=bi, ti=ti: e.scalar_tensor_tensor(
                            out=yo.ap[:, :], in0=bk.ap[:, :], scalar=rtok[:, ti, bi:bi + 1],
                            in1=g_bc[:, half * 512:(half + 1) * 512], op0=ALU.mult, op1=ALU.mult),
                           reads=(bk, R_rtok[ti], R_gbc), writes=(yo,))
                        dma("sp", lambda e, yo=yo, ydst=ydst, half=half, ysl=ysl: e.dma_start(
                            out=ysl(ydst, half), in_=yo.ap[:, :]),
                            reads=(yo,), track=yo, final=True)

            if st == 0:
                setup_gbc()
            for ti in range(ntile - 1):
                store_tile(st, ti)
            if st < len(STS) - 1:
                ensure_t(ntile - 1)
                stash.append(lambda st=st, ti=ntile - 1, f=store_tile: f(st, ti))
            else:
                store_tile(st, ntile - 1)
            flush_all()

        assert ws.loaded == len(ws.specs)
        spq = pg.eng["sp"]
        for (s, v) in pg.final_events.values():
            spq.q.append(lambda eng, s=s, v=v: eng.wait_ge(s, v))

        with nc.Block() as block:
            @block.tensor
            def _(e):
                for f in pg.eng["pe"].q:
                    f(e)

            @block.scalar
            def _(e):
                for f in pg.eng["act"].q:
                    f(e)

            @block.vector
            def _(e):
                for f in pg.eng["dve"].q:
                    f(e)

            @block.gpsimd
            def _(e):
                for f in pg.eng["pool"].q:
                    f(e)

            @block.sync
            def _(e):
                for f in pg.eng["sp"].q:
                    f(e)
    return nc


_NC_CACHE = {}


def kernel(x_prompt, x_sample, state_conv, state_pool, p_prompt, p_sample, g_mix, w_in,
           w_conv, w_out_conv, w_pool, pool_scale, w_o, g_mlp, w_up, w_down, g_ple,
           w_ple_gate, w_ple_proj, g_final):
    f = lambda a: np.ascontiguousarray(np.asarray(a, dtype=np.float32))
    if "nc" not in _NC_CACHE:
        _NC_CACHE["nc"] = build_program()
    nc = _NC_CACHE["nc"]
    shared = {
        "g_mix": f(g_mix).reshape(1, D), "g_mlp": f(g_mlp).reshape(1, D), "g_ple": f(g_ple).reshape(1, D),
        "g_final": f(g_final).reshape(1, D), "pool_scale": f(pool_scale).reshape(1, D),
        "w_conv": f(w_conv).reshape(3, D), "w_in": f(w_in).reshape(D, 6 * D),
        "w_out_conv": f(w_out_conv).reshape(D, D), "w_pool": f(w_pool).reshape(4, 256, 256),
        "w_o": f(w_o).reshape(D, D), "w_up": f(w_up).reshape(D, DFF), "w_down": f(w_down).reshape(DFF, D),
        "w_ple_gate": f(w_ple_gate).reshape(D, D), "w_ple_proj": f(w_ple_proj).reshape(PLE, D),
    }
    xpf, xsf = f(x_prompt), f(x_sample)
    ppf, psf = f(p_prompt)[0], f(p_sample)[0]
    scf, spf = f(state_conv)[0], f(state_pool)[0]
    in_maps = []
    for c in range(NCORE):
        m = dict(shared)
        sl = slice(c * NSEQ, (c + 1) * NSEQ)
        m["xp"] = xpf[c]
        m["xs"] = xsf[sl].reshape(NSAMP, D)
        m["pp"] = ppf[c]
        m["ps"] = psf[sl].reshape(NSAMP, PLE)
        m["sc"] = scf[sl].reshape(NSEQ * 2, D)
        m["sp"] = spf[sl].reshape(NSEQ * 15, D)
        in_maps.append(m)
    res = run_bass_kernel_spmd(nc, in_maps, core_ids=list(range(NCORE)))
    r = res.results
    y_prompt = np.stack([r[c]["yp"] for c in range(NCORE)]).astype(np.float32)
    y_sample = np.concatenate([r[c]["ys"].reshape(NSEQ, DSEQ, D) for c in range(NCORE)]).astype(np.float32)
    ncp_ = np.stack([r[c]["ncp"] for c in range(NCORE)])[None].astype(np.float32)
    npp_ = np.stack([r[c]["npp"] for c in range(NCORE)])[None].astype(np.float32)
    ncs_ = np.concatenate([r[c]["ncs"].reshape(NSEQ, 2, D) for c in range(NCORE)])[None].astype(np.float32)
    nps_ = np.concatenate([r[c]["nps"].reshape(NSEQ, 15, D) for c in range(NCORE)])[None].astype(np.float32)
    return (y_prompt, y_sample, ncp_, npp_, ncs_, nps_)
```

```python
import numpy as np
from contextlib import ExitStack

import concourse.bass as bass
import concourse.mybir as mybir
from concourse.bass_utils import run_bass_kernel_spmd

F32 = mybir.dt.float32
BF16 = mybir.dt.bfloat16
AF = mybir.ActivationFunctionType
ALU = mybir.AluOpType

P = 128
D = 1024
KC = 8
NCORE = 8
SEQ = 2048
NSEQ = 16
DSEQ = 8
NSAMP = NSEQ * DSEQ
PLE = 256
DFF = 4096
EPS = 1e-6
NTMAX = 768
WINDOWS = (2, 4, 8, 16)

STS = [
    (0, 768, False, [("p", 0, 384), ("p", 384, 768)]),
    (768, 768, False, [("p", 0, 384), ("p", 384, 768)]),
    (1536, 512, True, [("p", 0, 256), ("m", 256, 640)]),
]
TW = 384

CV_GMIX, CV_GMLP, CV_GPLE, CV_GFIN, CV_PSCALE, CV_WCONV = 0, 8, 16, 24, 32, 40


class Res:
    __slots__ = ("name", "last_w", "readers", "gen", "dsem", "dcount", "excl")

    def __init__(self, name):
        self.name = name
        self.excl = False
        self.last_w = None
        self.readers = {}
        self.gen = 0
        self.dsem = None
        self.dcount = 0


class H:
    __slots__ = ("res", "gen", "ap")

    def __init__(self, res, ap):
        self.res = res
        self.gen = res.gen
        self.ap = ap


def _res(x):
    if isinstance(x, H):
        assert x.res.gen == x.gen, f"stale ring handle {x.res.name}"
        return x.res
    return x


class Eng:
    def __init__(self, name, sem):
        self.name = name
        self.sem = sem
        self.count = 0
        self.waited = {}
        self.q = []


class Prog:
    def __init__(self, nc, stack):
        self.nc = nc
        self.stack = stack
        self.eng = {}
        for n in ("pe", "act", "dve", "pool", "sp"):
            s = stack.enter_context(nc.semaphore("sem_" + n))
            self.eng[n] = Eng(n, s)
        self.nsem = 0
        self.final_events = {}

    def new_sem(self, name):
        self.nsem += 1
        return self.stack.enter_context(self.nc.semaphore(f"ds{self.nsem}_{name}"))

    def _deps(self, reads, writes):
        deps = {}

        def add(ev):
            if ev is None:
                return
            s, v = ev
            k = id(s)
            if k not in deps or deps[k][1] < v:
                deps[k] = (s, v)

        for r in reads:
            r = _res(r)
            add(r.last_w)
            if r.excl:
                for ev in r.readers.values():
                    add(ev)
        for w in writes:
            w = _res(w)
            add(w.last_w)
            for ev in w.readers.values():
                add(ev)
        return deps

    def _emit_waits(self, e, deps, skip_self=False):
        for k, (s, v) in deps.items():
            if skip_self and s is e.sem:
                continue
            if e.waited.get(k, 0) >= v:
                continue
            e.waited[k] = v
            e.q.append(lambda eng, s=s, v=v: eng.wait_ge(s, v))

    def _record(self, ev, reads, writes):
        s, v = ev
        for r in reads:
            r = _res(r)
            k = id(s)
            if k not in r.readers or r.readers[k][1] < v:
                r.readers[k] = ev
        for w in writes:
            w = _res(w)
            w.last_w = ev
            w.readers = {}

    def op(self, en, fn, reads=(), writes=()):
        e = self.eng[en]
        deps = self._deps(reads, writes)
        self._emit_waits(e, deps, skip_self=(en == "pe"))
        e.count += 1
        ev = (e.sem, e.count)

        def run(eng, fn=fn, sem=e.sem):
            ins = fn(eng)
            ins.then_inc(sem, 1)

        e.q.append(run)
        self._record(ev, reads, writes)
        return ev

    def dma(self, en, fn, reads=(), writes=(), track=None, final=False, nodeps=False):
        e = self.eng[en]
        deps = self._deps(reads, writes)
        if not nodeps:
            self._emit_waits(e, deps)
        t = _res(track)
        if t.dsem is None:
            t.dsem = self.new_sem(t.name)
        t.dcount += 16
        ev = (t.dsem, t.dcount)

        def run(eng, fn=fn, sem=t.dsem):
            ins = fn(eng)
            ins.then_inc(sem, 16)

        e.q.append(run)
        self._record(ev, reads, writes)
        if final:
            self.final_events[id(t.dsem)] = ev
        return ev


class Ring:
    def __init__(self, name, aps):
        self.slots = [(Res(f"{name}{i}"), ap) for i, ap in enumerate(aps)]
        self.i = 0

    def alloc(self):
        r, ap = self.slots[self.i % len(self.slots)]
        self.i += 1
        r.gen += 1
        return H(r, ap)


def build_program():
    nc = bass.Bass("TRN2", target_bir_lowering=False)

    def din(name, shape):
        return nc.dram_tensor(name, shape, F32, kind="ExternalInput").ap()

    def dout(name, shape):
        return nc.dram_tensor(name, shape, F32, kind="ExternalOutput").ap()

    xp = din("xp", [SEQ, D]); xs = din("xs", [NSAMP, D])
    pp = din("pp", [SEQ, PLE]); ps_ = din("ps", [NSAMP, PLE])
    sc = din("sc", [NSEQ * 2, D]); sp = din("sp", [NSEQ * 15, D])
    g_mix = din("g_mix", [1, D]); g_mlp = din("g_mlp", [1, D]); g_ple = din("g_ple", [1, D])
    g_final = din("g_final", [1, D]); pool_scale = din("pool_scale", [1, D])
    w_conv = din("w_conv", [3, D])
    w_in = din("w_in", [D, 6 * D]); w_out_conv = din("w_out_conv", [D, D])
    w_pool = din("w_pool", [4, 256, 256]); w_o = din("w_o", [D, D])
    w_up = din("w_up", [D, DFF]); w_down = din("w_down", [DFF, D])
    w_ple_gate = din("w_ple_gate", [D, D]); w_ple_proj = din("w_ple_proj", [PLE, D])
    yp = dout("yp", [SEQ, D]); ys = dout("ys", [NSAMP, D])
    ncp = dout("ncp", [2, D]); npp = dout("npp", [15, D])
    ncs = dout("ncs", [NSEQ * 2, D]); nps = dout("nps", [NSEQ * 15, D])

    with ExitStack() as stack:
        def sb(name, shape, dt):
            return stack.enter_context(nc.sbuf_tensor(name, shape, dt))

        hT = sb("hT", [P, KC, NTMAX], F32)
        xn = sb("xn", [P, KC, NTMAX], BF16)
        ua = sb("ua", [P, KC, NTMAX], BF16)
        mg = sb("mg", [P, KC, NTMAX], BF16)
        pooled = sb("pooled", [P, KC, NTMAX], BF16)
        pT = sb("pT", [P, 2, NTMAX], BF16)
        NEXT = 2
        u_ext = [sb(f"u_ext{i}", [P, 2 + NTMAX], F32) for i in range(NEXT)]
        u_exs = [sb(f"u_exs{i}", [P, 10 * NSEQ], F32) for i in range(NEXT)]
        v_ext = [sb(f"v_ext{i}", [P, 15 + NTMAX], F32) for i in range(NEXT)]
        v_exs = [sb(f"v_exs{i}", [P, 23 * NSEQ], F32) for i in range(NEXT)]
        sAp = sb("sAp", [P, 15 + NTMAX], F32); sAs = sb("sAs", [P, 23 * NSEQ], F32)
        sBp = sb("sBp", [P, 15 + NTMAX], F32); sBs = sb("sBs", [P, 23 * NSEQ], F32)
        NW = 12
        wring_t = [sb(f"wr{i}", [P, 2048], BF16) for i in range(NW)]
        wpool_sb = sb("wpool_sb", [P, 4, 2, 256], BF16)
        pproj_sb = sb("pproj_sb", [P, 2, D], BF16)
        NTF = 9
        tmpF_t = [sb(f"tf{i}", [P, TW], F32) for i in range(NTF)]
        NTB = 11
        tmpB_t = [sb(f"tb{i}", [P, TW], BF16) for i in range(NTB)]
        sqbuf = sb("sqbuf", [P, KC, TW], BF16)
        yo_t = [sb(f"yo{i}", [P, 512], F32) for i in range(6)]
        stg_t = [sb(f"stg{i}", [P, TW], F32) for i in range(2)]
        ident = sb("ident", [P, P], F32)
        ones_b = sb("ones_b", [P, P], BF16)
        cstage = sb("cstage", [P, P], F32)
        cvec = sb("cvec", [P, 64], F32)
        inv_t = sb("inv_t", [P, 15], F32)
        invcnt = sb("invcnt", [P, 4, 15], F32)
        g_bc = sb("g_bc", [P, D], F32)
        rtok = sb("rtok", [P, 3, 4], F32)
        u_hist = sb("u_hist", [P, KC, 2], F32)
        v_hist = sb("v_hist", [P, KC, 15], F32)
        banks_t = [stack.enter_context(nc.psum_tensor(f"bank{i}", [P, 512], F32)) for i in range(8)]

        pg = Prog(nc, stack)
        op, dma = pg.op, pg.dma

        R_hT = [[Res(f"hT{j}_{t}") for t in range(3)] for j in range(KC)]
        R_xn = [[Res(f"xn{j}_{t}") for t in range(3)] for j in range(KC)]
        R_ua = [[Res(f"ua{j}_{t}") for t in range(3)] for j in range(KC)]
        R_mg = [[Res(f"mg{j}_{t}") for t in range(3)] for j in range(KC)]
        R_pl = [[Res(f"pl{j}_{t}") for t in range(3)] for j in range(KC)]
        R_pT = [Res(f"pT{t}") for t in range(3)]
        R_uext = [Res(f"uext{i}") for i in range(NEXT)]
        R_vext = [Res(f"vext{i}") for i in range(NEXT)]
        R_sA = Res("sA"); R_sB = Res("sB"); R_sAs = Res("sAs"); R_sBs = Res("sBs")
        R_sq = Res("sq")
        R_ident = Res("ident"); R_ones = Res("ones"); R_cvec = Res("cvec"); R_cstage = [Res(f"cstage{i}") for i in range(6)]
        R_invt = Res("inv_t"); R_invcnt = Res("invcnt")
        R_uh = [Res(f"uh{j}") for j in range(KC)]
        R_vh = [Res(f"vh{j}") for j in range(KC)]
        R_wpool = Res("wpool"); R_pproj = Res("pproj")
        R_gbc = Res("g_bc"); R_onesf = Res("ones_f"); R_rtok = [Res(f"rtok{t}") for t in range(3)]
        R_dd = Res("dram2dram")

        banks = Ring("bank", [b for b in banks_t])
        for r_, _ in banks.slots:
            r_.excl = True
        tmpF = Ring("tf", [t for t in tmpF_t])
        tmpB = Ring("tb", [t for t in tmpB_t])
        yor = Ring("yo", [t for t in yo_t])
        stgr = Ring("stg", [t for t in stg_t])

        class WStream:
            def __init__(self):
                self.res = [Res(f"wslot{i}") for i in range(NW)]
                self.specs = []
                self.loaded = 0
                self.done_upto = 0
                self.done = []
                self.first_reads = ()

            def add(self, spec):
                self.specs.append(spec)
                self.done.append(False)
                return len(self.specs) - 1

            def pump(self):
                while self.loaded < len(self.specs):
                    n = self.loaded
                    if n >= NW and not self.done[n - NW]:
                        break
                    src, kind = self.specs[n]
                    slot = wring_t[n % NW]
                    r = self.res[n % NW]
                    if kind == "k8":
                        dst = slot[:, :].rearrange("p (k c) -> p k c", k=8)
                        s = src.rearrange("(k p) c -> p k c", p=P)
                    else:
                        raise AssertionError(kind)
                    first_rd = self.first_reads if n == 0 else ()
                    dma("pool", lambda e, dst=dst, s=s: e.dma_start(out=dst, in_=s),
                        reads=first_rd, writes=(r,), track=r)
                    self.loaded += 1

            def view(self, n):
                assert n < self.loaded, "weight tile not yet loaded (ring too small)"
                assert n >= self.loaded - NW
                return wring_t[n % NW][:, :].rearrange("p (k c) -> p k c", k=8), self.res[n % NW]

            def finish(self, n):
                self.done[n] = True
                self.pump()

        ws = WStream()

        def wtile(src2d, c0):
            return ws.add((src2d[:, c0:c0 + 256], "k8"))

        pending = []

        def after_groups(k, key, fn):
            pending.append([k, key, fn])

        def tick():
            fire = [p for p in pending if p[0] <= 1]
            for p in pending:
                p[0] -= 1
            for p in fire:
                pending.remove(p)
            for p in fire:
                p[2]()

        def ensure(key):
            for p in list(pending):
                if p[1] == key:
                    pending.remove(p)
                    p[2]()

        def ensure_t(ti):
            for p in list(pending):
                if p in pending and p[1][1] == ti:
                    pending.remove(p)
                    p[2]()

        def flush_all():
            while pending:
                p = pending.pop(0)
                p[2]()

        def mm_group(bank, pairs, reads, n, do_tick=True):
            def fn(e, pairs=pairs, bank=bank, n=n):
                last = None
                L = len(pairs)
                for i, (l, r) in enumerate(pairs):
                    last = e.matmul(bank.ap[:, 0:n], lhsT=l, rhs=r, start=(i == 0), stop=(i == L - 1))
                return last
            ev = op("pe", fn, reads=reads, writes=(bank,))
            if do_tick:
                tick()
            return ev

        def pe_group(fn, reads, writes):
            ev = op("pe", fn, reads=reads, writes=writes)
            tick()
            return ev

        op("pool", lambda e: e.memset(ident[:, :], 0.0), writes=(R_ident,))
        op("pool", lambda e: e.affine_select(out=ident[:, :], in_=ident[:, :], pattern=[[-1, P]],
                                               compare_op=ALU.not_equal, fill=1.0, base=0,
                                               channel_multiplier=1),
           reads=(), writes=(R_ident,))
        op("dve", lambda e: e.memset(ones_b[:, :], 1.0), writes=(R_ones,))
        vecs = [(g_mix, CV_GMIX), (g_mlp, CV_GMLP), (g_ple, CV_GPLE), (g_final, CV_GFIN), (pool_scale, CV_PSCALE)]
        for i, (v, c) in enumerate(vecs):
            dma("sp", lambda e, v=v, c=c: e.dma_start(out=cstage[c:c + 8, :],
                                                    in_=v.rearrange("o (j p) -> (o j) p", p=P)),
                writes=(R_cstage[i],), track=R_cstage[i])
        dma("sp", lambda e: e.dma_start(out=cstage[CV_WCONV:CV_WCONV + 24, :],
                                        in_=w_conv.rearrange("k (j p) -> (k j) p", p=P)),
            writes=(R_cstage[5],), track=R_cstage[5])
        dma("pool", lambda e: e.dma_start(out=wpool_sb[:, :, :, :],
                                          in_=w_pool.rearrange("g (k p) d -> p g k d", p=P)),
            writes=(R_wpool,), track=R_wpool)
        dma("pool", lambda e: e.dma_start(out=pproj_sb[:, :, :],
                                          in_=w_ple_proj.rearrange("(k p) d -> p k d", p=P)),
            writes=(R_pproj,), track=R_pproj)
        bk = banks.alloc()
        op("pe", lambda e, bk=bk: e.transpose(bk.ap[:, 0:64], cstage[0:64, :], ident[0:64, 0:64]),
           reads=R_cstage + [R_ident], writes=(bk,))
        op("dve", lambda e, bk=bk: e.tensor_copy(out=cvec[:, :], in_=bk.ap[:, 0:64]),
           reads=(bk,), writes=(R_cvec,))
        op("dve", lambda e: e.memset(cstage[:, :], 1.0), writes=R_cstage + [R_onesf])
        for t in range(15):
            op("dve", lambda e, t=t: e.memset(inv_t[:, t:t + 1], 1.0 / (t + 1)), writes=(R_invt,))
        for g, w in enumerate(WINDOWS):
            op("dve", lambda e, g=g, w=w: e.tensor_scalar(out=invcnt[:, g, :], in0=inv_t[:, :], scalar1=1.0 / w,
                                                        scalar2=None, op0=ALU.max),
               reads=(R_invt,), writes=(R_invcnt,))
        def cv(col):
            return cvec[:, col:col + 1]

        grs = []
        for half in range(2):
            gr = yor.alloc()
            dma("sp", lambda e, gr=gr, half=half: e.dma_start(out=gr.ap[0:1, :], in_=g_final[0:1, half * 512:(half + 1) * 512]),
                writes=(gr,), track=gr)
            grs.append(gr)

        def setup_gbc():
          for half in range(2):
              gr = grs[half]
              bk = banks.alloc()
              op("pe", lambda e, bk=bk, gr=gr: e.matmul(bk.ap[:, :], lhsT=cstage[0:1, :], rhs=gr.ap[0:1, :], start=True, stop=True),
                 reads=(gr, R_onesf), writes=(bk,))
              op("dve", lambda e, bk=bk, half=half: e.tensor_copy(out=g_bc[:, half * 512:(half + 1) * 512], in_=bk.ap[:, :]),
                 reads=(bk,), writes=(R_gbc,))

        def norm_A(st, ti):
            _, c0, c1 = STS[st][3][ti]
            n = c1 - c0
            for j in range(KC):
                op("act", lambda e, j=j: e.activation(out=sqbuf[:, j, 0:n], in_=hT[:, j, c0:c1], func=AF.Square),
                   reads=(R_hT[j][ti],), writes=(R_sq,))

        def norm_BC(st, ti, gcol, key, inplace_f32=False, trickle=False):
            _, c0, c1 = STS[st][3][ti]
            n = c1 - c0
            bk = banks.alloc()
            mm_group(bk, [(ones_b[:, :], sqbuf[:, j, 0:n]) for j in range(KC)], reads=(R_sq, R_ones), n=n, do_tick=False)
            s1 = tmpF.alloc()
            op("act", lambda e: e.activation(out=s1.ap[:, 0:n], in_=bk.ap[:, 0:n], func=AF.Sqrt,
                                             bias=EPS, scale=1.0 / D),
               reads=(bk,), writes=(s1,))
            rb = tmpF.alloc()
            op("dve", lambda e: e.reciprocal(out=rb.ap[:, 0:n], in_=s1.ap[:, 0:n]), reads=(s1,), writes=(rb,))

            def apply(j):
                assert rb.res.gen == rb.gen, "norm scale slot recycled before use"
                if inplace_f32:
                    op("dve", lambda e, j=j: e.scalar_tensor_tensor(out=hT[:, j, c0:c1], in0=hT[:, j, c0:c1],
                                                                  scalar=cv(gcol + j), in1=rb.ap[:, 0:n],
                                                                  op0=ALU.mult, op1=ALU.mult),
                       reads=(rb, R_cvec), writes=(R_hT[j][ti],))
                else:
                    op("dve", lambda e, j=j: e.scalar_tensor_tensor(out=xn[:, j, c0:c1], in0=hT[:, j, c0:c1],
                                                                  scalar=cv(gcol + j), in1=rb.ap[:, 0:n],
                                                                  op0=ALU.mult, op1=ALU.mult),
                       reads=(R_hT[j][ti], rb, R_cvec), writes=(R_xn[j][ti],))
            for j in range(KC):
                if trickle:
                    after_groups(j + 1, key, lambda j=j: apply(j))
                else:
                    apply(j)

        def final_norm_B(st, ti):
            _, c0, c1 = STS[st][3][ti]
            nb = (c1 - c0) // P
            bk = banks.alloc()

            def fn(e):
                last = None
                for b in range(nb):
                    for j in range(KC):
                        last = e.matmul(bk.ap[:, b:b + 1], lhsT=sqbuf[:, j, b * P:(b + 1) * P], rhs=ones_b[:, 0:1],
                                        start=(j == 0), stop=(j == KC - 1))
                return last
            op("pe", fn, reads=(R_sq, R_ones), writes=(bk,))
            s1 = tmpF.alloc()
            op("act", lambda e: e.activation(out=s1.ap[:, 0:nb], in_=bk.ap[:, 0:nb], func=AF.Sqrt, bias=EPS, scale=1.0 / D),
               reads=(bk,), writes=(s1,))
            op("dve", lambda e: e.reciprocal(out=rtok[:, ti, 0:nb], in_=s1.ap[:, 0:nb]), reads=(s1,), writes=(R_rtok[ti],))

        def final_norm_deferred(st, ti, key, k=13):
            norm_A(st, ti)
            after_groups(k, key, lambda: final_norm_B(st, ti))

        def norm_deferred(st, ti, gcol, key, inplace_f32=False, k=3, trickle=False):
            norm_A(st, ti)
            after_groups(k, key, lambda: norm_BC(st, ti, gcol, key, inplace_f32, trickle))

        ua_f = ua[:, :, :].rearrange("p k n -> p (k n)").bitcast(F32)
        mg_f = mg[:, :, :].rearrange("p k n -> p (k n)").bitcast(F32)
        pl_f = pooled[:, :, :].rearrange("p k n -> p (k n)").bitcast(F32)
        CHB = NTMAX * 2

        def cover(R, lo, hi):
            return [R[k][t] for k in range(lo // CHB, (hi - 1) // CHB + 1) for t in range(3)]

        def xstage(bi):
            buf, R = (ua_f, R_ua) if bi < 3 else (mg_f, R_mg)
            i = bi % 3
            return buf[:, i * D:(i + 1) * D], cover(R, i * D * 4, (i + 1) * D * 4)

        def pstage(bi):
            return pl_f[:, bi * PLE:(bi + 1) * PLE], cover(R_pl, bi * PLE * 4, (bi + 1) * PLE * 4)

        def prefetch_inputs(st, which, nodeps=False):
            p0_, NP_, _, tiles_ = STS[st]
            nblk = tiles_[-1][2] // P
            for bi in range(nblk):
                b0 = bi * P
                if b0 < NP_:
                    xsrc, psrc = xp[p0_ + b0:p0_ + b0 + P, :], pp[p0_ + b0:p0_ + b0 + P, :]
                else:
                    assert b0 == NP_
                    xsrc = xs.rearrange("(s t) d -> t s d", t=DSEQ)
                    psrc = ps_.rearrange("(s t) d -> t s d", t=DSEQ)
                if (which == "ua" and bi < 3) or (which == "mg" and bi >= 3):
                    dst, rs = xstage(bi)
                    dma("sp", lambda e, dst=dst, xsrc=xsrc: e.dma_start(out=dst, in_=xsrc), writes=rs, track=rs[0],
                        nodeps=nodeps)
                if which == "ua":
                    dst, rs = pstage(bi)
                    dma("sp", lambda e, dst=dst, psrc=psrc: e.dma_start(out=dst, in_=psrc), writes=rs, track=rs[0],
                        nodeps=nodeps)

        prefetch_inputs(0, "ua", nodeps=True)
        prefetch_inputs(0, "mg", nodeps=True)
        ws.first_reads = (R_ua[0][0],)

        def state_in_dma(j):
            stg = stgr.alloc()
            dma("sp", lambda e, stg=stg, j=j: e.dma_start(out=stg.ap[0:120, 0:P], in_=sp[0:120, j * P:(j + 1) * P]),
                writes=(stg,), track=stg)
            dma("sp", lambda e, stg=stg, j=j: e.dma_start(out=stg.ap[0:120, P:2 * P], in_=sp[120:240, j * P:(j + 1) * P]),
                writes=(stg,), track=stg, nodeps=True)
            dma("sp", lambda e, stg=stg, j=j: e.dma_start(out=stg.ap[0:32, 2 * P:3 * P], in_=sc[0:32, j * P:(j + 1) * P]),
                writes=(stg,), track=stg, nodeps=True)
            return stg

        stash = []
        for st, (p0, NP, has_s, tiles) in enumerate(STS):
            NT = tiles[-1][2]
            ntile = len(tiles)
            ptiles = [ti for ti in range(ntile) if tiles[ti][0] in ("p", "m")]
            stiles = [ti for ti in range(ntile) if tiles[ti][0] in ("s", "m")]
            W1a = []
            for jp in range(4):
                W1a.append([wtile(w_in, off + jp * 256) for off in (0, 1024, 2048, 3072)])
            W1b = []
            for jp in range(4):
                W1b.append([wtile(w_out_conv, jp * 256), wtile(w_in, 4096 + jp * 256), wtile(w_in, 5120 + jp * 256)])
            W2 = [wtile(w_o, ip * 256) for ip in range(4)]
            W4 = []
            for q in range(4):
                ups = [wtile(w_up, q * 1024 + fp * 256) for fp in range(4)]
                dns = [wtile(w_down[q * 1024:(q + 1) * 1024, :], jp * 256) for jp in range(4)]
                W4.append((ups, dns))
            W5 = [wtile(w_ple_gate, jp * 256) for jp in range(4)]
            ws.pump()
            stgs = {}
            if has_s:
                stgs[0] = state_in_dma(0)

            for ti, (kind, c0, c1) in enumerate(tiles):
                n = c1 - c0
                for b0 in range(c0, c1, P):
                    xin_ap, xin_rs = xstage(b0 // P)
                    pin_ap, pin_rs = pstage(b0 // P)
                    for half in range(2):
                        bk = banks.alloc()

                        def fn(e, bk=bk, xin_ap=xin_ap, half=half):
                            last = None
                            for jj in range(4):
                                j = half * 4 + jj
                                last = e.transpose(bk.ap[:, jj * P:(jj + 1) * P], xin_ap[:, j * P:(j + 1) * P], ident[:, :])
                            return last
                        pe_group(fn, reads=xin_rs + [R_ident], writes=(bk,))
                        bv = bk.ap[:, :].rearrange("p (j t) -> p j t", j=4)
                        op("act", lambda e, bv=bv, half=half, b0=b0: e.activation(
                            out=hT[:, half * 4:half * 4 + 4, b0:b0 + P], in_=bv, func=AF.Copy),
                           reads=(bk,), writes=[R_hT[half * 4 + jj][ti] for jj in range(4)])
                        op("act", lambda e, bv=bv, half=half, b0=b0, c0=c0: e.activation(
                            out=sqbuf[:, half * 4:half * 4 + 4, b0 - c0:b0 - c0 + P], in_=bv, func=AF.Square),
                           reads=(bk,), writes=(R_sq,))
                    bk = banks.alloc()

                    def fnp(e, bk=bk, pin_ap=pin_ap):
                        last = None
                        for k in range(2):
                            last = e.transpose(bk.ap[:, k * P:(k + 1) * P], pin_ap[:, k * P:(k + 1) * P], ident[:, :])
                        return last
                    pe_group(fnp, reads=pin_rs + [R_ident], writes=(bk,))
                    op("dve", lambda e, bk=bk, b0=b0: e.tensor_copy(
                        out=pT[:, :, b0:b0 + P], in_=bk.ap[:, 0:2 * P].rearrange("p (k t) -> p k t", k=2)),
                       reads=(bk,), writes=(R_pT[ti],))
                flush_all()
                norm_BC(st, ti, CV_GMIX, ("n1", ti))
                if ti == 0 and stash:
                    stash.pop()()

            def pre_chunk(j, stg):
                xi = j % NEXT
                ue, us_, ve, vs_ = u_ext[xi], u_exs[xi], v_ext[xi], v_exs[xi]
                Ru, Rv = R_uext[xi], R_vext[xi]
                if st == 0:
                    op("dve", lambda e, ue=ue: e.memset(ue[:, 0:2], 0.0), writes=(Ru,))
                    op("dve", lambda e, ve=ve: e.memset(ve[:, 0:15], 0.0), writes=(Rv,))
                else:
                    op("dve", lambda e, ue=ue, j=j: e.tensor_copy(out=ue[:, 0:2], in_=u_hist[:, j, :]),
                       reads=(R_uh[j],), writes=(Ru,))
                    op("dve", lambda e, ve=ve, j=j: e.tensor_copy(out=ve[:, 0:15], in_=v_hist[:, j, :]),
                       reads=(R_vh[j],), writes=(Rv,))
                if has_s:
                    bk = banks.alloc()

                    def fns(e, bk=bk, stg=stg):
                        e.transpose(bk.ap[:, 0:120], stg.ap[0:120, 0:P], ident[0:120, 0:120])
                        e.transpose(bk.ap[:, 128:248], stg.ap[0:120, P:2 * P], ident[0:120, 0:120])
                        return e.transpose(bk.ap[:, 256:288], stg.ap[0:32, 2 * P:3 * P], ident[0:32, 0:32])
                    pe_group(fns, reads=(stg, R_ident), writes=(bk,))
                    vsv = vs_[:, 0:15 * NSEQ].rearrange("p (h s) -> p s h", s=NSEQ)
                    usv = us_[:, 0:2 * NSEQ].rearrange("p (h s) -> p s h", s=NSEQ)
                    op("act", lambda e, bk=bk, vsv=vsv: e.activation(
                        out=vsv[:, 0:8, :], in_=bk.ap[:, 0:120].rearrange("p (s h) -> p s h", h=15), func=AF.Copy),
                       reads=(bk,), writes=(Rv,))
                    op("act", lambda e, bk=bk, vsv=vsv: e.activation(
                        out=vsv[:, 8:16, :], in_=bk.ap[:, 128:248].rearrange("p (s h) -> p s h", h=15), func=AF.Copy),
                       reads=(bk,), writes=(Rv,))
                    op("act", lambda e, bk=bk, usv=usv: e.activation(
                        out=usv, in_=bk.ap[:, 256:288].rearrange("p (s h) -> p s h", h=2), func=AF.Copy),
                       reads=(bk,), writes=(Ru,))

            def mm_chunk(j, wv):
                jc = (j % 2) * P
                xi = j % NEXT
                ue, us_, ve, vs_ = u_ext[xi], u_exs[xi], v_ext[xi], v_exs[xi]
                Ru, Rv = R_uext[xi], R_vext[xi]
                for ti, (kind, c0, c1) in enumerate(tiles):
                    n = c1 - c0
                    ensure_t(ti)
                    xr = [R_xn[k][ti] for k in range(KC)]
                    bks = []
                    for wi in range(4):
                        wap, wr = wv[wi]
                        bk = banks.alloc()
                        mm_group(bk, [(wap[:, k, jc:jc + P], xn[:, k, c0:c1]) for k in range(KC)],
                                 reads=xr + [wr], n=n)
                        bks.append(bk)
                    Bb, Bc, Bh, Bv = bks
                    if kind == "p":
                        sgs = [("p", c0, c1)]
                    elif kind == "s":
                        sgs = [("s", c0, c1)]
                    else:
                        sgs = [("p", c0, NP), ("s", NP, c1)]
                    csb = tmpF.alloc()
                    op("act", lambda e, csb=csb, Bc=Bc, n=n: e.activation(out=csb.ap[:, 0:n], in_=Bc.ap[:, 0:n], func=AF.Copy),
                       reads=(Bc,), writes=(csb,))
                    y3 = tmpF.alloc()
                    for (sk, a0, a1) in sgs:
                        r0, r1 = a0 - c0, a1 - c0
                        if sk == "p":
                            def uo(s, ue=ue, a0=a0, a1=a1):
                                return ue[:, 2 + a0 - s:2 + a1 - s]
                            vdst = ve[:, 15 + a0:15 + a1]

                            def vw(a):
                                return a
                        else:
                            def uo(s, us_=us_):
                                return us_[:, (2 - s) * NSEQ:(10 - s) * NSEQ]
                            vdst = vs_[:, 15 * NSEQ:23 * NSEQ]

                            def vw(a):
                                return a
                        op("dve", lambda e, csb=csb, Bh=Bh, r0=r0, r1=r1, uo=uo, vw=vw: e.tensor_tensor(
                            out=uo(0), in0=vw(csb.ap[:, r0:r1]), in1=vw(Bh.ap[:, r0:r1]), op=ALU.mult),
                           reads=(csb, Bh), writes=(Ru,))
                        op("act", lambda e, Bc=Bc, r0=r0, r1=r1, uo=uo, vw=vw, j=j: e.activation(
                            out=vw(Bc.ap[:, r0:r1]), in_=uo(0), func=AF.Copy, scale=cv(CV_WCONV + 16 + j)),
                           reads=(Ru, R_cvec), writes=(Bc,))
                        op("dve", lambda e, Bc=Bc, Bh=Bh, r0=r0, r1=r1, uo=uo, vw=vw, j=j: e.scalar_tensor_tensor(
                            out=vw(Bh.ap[:, r0:r1]), in0=uo(1), scalar=cv(CV_WCONV + 8 + j), in1=vw(Bc.ap[:, r0:r1]),
                            op0=ALU.mult, op1=ALU.add),
                           reads=(Ru, Bc, R_cvec), writes=(Bh,))
                        op("dve", lambda e, Bh=Bh, y3=y3, r0=r0, r1=r1, uo=uo, vw=vw, j=j: e.scalar_tensor_tensor(
                            out=vw(y3.ap[:, r0:r1]), in0=uo(2), scalar=cv(CV_WCONV + j), in1=vw(Bh.ap[:, r0:r1]),
                            op0=ALU.mult, op1=ALU.add),
                           reads=(Ru, Bh, R_cvec), writes=(y3,))
                        op("act", lambda e, Bv=Bv, r0=r0, r1=r1, vdst=vdst, vw=vw: e.activation(
                            out=vdst, in_=vw(Bv.ap[:, r0:r1]), func=AF.Copy),
                           reads=(Bv,), writes=(Rv,))
                    op("dve", lambda e, y3=y3, Bb=Bb, n=n, j=j, c0=c0, c1=c1: e.tensor_tensor(
                        out=ua[:, j, c0:c1], in0=Bb.ap[:, 0:n], in1=y3.ap[:, 0:n], op=ALU.mult),
                       reads=(Bb, y3), writes=(R_ua[j][ti],))

            def pool_chunk(j):
                xi = j % NEXT
                ue, us_, ve, vs_ = u_ext[xi], u_exs[xi], v_ext[xi], v_exs[xi]
                Ru, Rv = R_uext[xi], R_vext[xi]
                g = j // 2
                w = WINDOWS[g]
                segs = [("p", ve, sAp, sBp, 15 + NP, R_sA, R_sB)]
                if has_s:
                    segs.append(("s", vs_, sAs, sBs, 23, R_sAs, R_sBs))
                finals = []
                for (kind, vb, A, B, L, RA, RB) in segs:
                    if kind == "p":
                        def sl(buf, a, b):
                            return buf[:, a:b]
                    else:
                        def sl(buf, a, b):
                            return buf[:, a * NSEQ:b * NSEQ]
                    op("pool", lambda e, sl=sl, vb=vb, A=A, L=L: e.tensor_tensor(
                        out=sl(A, 1, L), in0=sl(vb, 1, L), in1=sl(vb, 0, L - 1), op=ALU.add),
                       reads=(Rv,), writes=(RA,))
                    S, RS = A, RA
                    if w >= 4:
                        op("pool", lambda e, sl=sl, A=A, B=B, L=L: e.tensor_tensor(
                            out=sl(B, 3, L), in0=sl(A, 3, L), in1=sl(A, 1, L - 2), op=ALU.add),
                           reads=(RA,), writes=(RB,))
                        S, RS = B, RB
                    if w >= 8:
                        op("pool", lambda e, sl=sl, A=A, B=B, L=L: e.tensor_tensor(
                            out=sl(A, 7, L), in0=sl(B, 7, L), in1=sl(B, 3, L - 4), op=ALU.add),
                           reads=(RB,), writes=(RA,))
                        S, RS = A, RA
                    if w >= 16:
                        op("pool", lambda e, sl=sl, A=A, B=B, L=L: e.tensor_tensor(
                            out=sl(B, 15, L), in0=sl(A, 15, L), in1=sl(A, 7, L - 8), op=ALU.add),
                           reads=(RA,), writes=(RB,))
                        S, RS = B, RB
                    if kind == "p":
                        pdst = pooled[:, j, 0:NP]
                        wres = [R_pl[j][ti] for ti in ptiles]
                    else:
                        pdst = pooled[:, j, NP:NP + NSAMP]
                        wres = [R_pl[j][ti] for ti in stiles]
                    def fin(sl=sl, S=S, RS=RS, vb=vb, L=L, pdst=pdst, wres=wres, kind=kind):
                        op("dve", lambda e: e.scalar_tensor_tensor(
                            out=pdst, in0=sl(S, 15, L), scalar=1.0 / w, in1=sl(vb, 15, L),
                            op0=ALU.mult, op1=ALU.subtract),
                           reads=(RS, Rv), writes=wres)
                        if kind == "p" and st == 0:
                            fx = tmpF.alloc()
                            op("dve", lambda e, fx=fx: e.tensor_tensor(
                                out=fx.ap[:, 0:15], in0=S[:, 15:30], in1=invcnt[:, g, :], op=ALU.mult),
                               reads=(RS, R_invcnt), writes=(fx,))
                            op("dve", lambda e, fx=fx: e.tensor_tensor(
                                out=pooled[:, j, 0:15], in0=fx.ap[:, 0:15], in1=vb[:, 15:30], op=ALU.subtract),
                               reads=(fx, Rv), writes=(R_pl[j][0],))
                    finals.append(fin)

                def final():
                    for f in finals:
                        f()
                    if st < len(STS) - 1:
                        op("dve", lambda e: e.tensor_copy(out=u_hist[:, j, :], in_=ue[:, NPc:NPc + 2]),
                           reads=(Ru,), writes=(R_uh[j],))
                        op("dve", lambda e: e.tensor_copy(out=v_hist[:, j, :], in_=ve[:, NPc:NPc + 15]),
                           reads=(Rv,), writes=(R_vh[j],))
                NPc = NP
                return final

            def state_out(j):
                xi = j % NEXT
                ue, us_, ve, vs_ = u_ext[xi], u_exs[xi], v_ext[xi], v_exs[xi]
                Ru, Rv = R_uext[xi], R_vext[xi]
                bkA = banks.alloc()

                def fno(e, bkA=bkA, vs_=vs_, us_=us_):
                    e.transpose(bkA.ap[0:128, 0:P], vs_[:, 8 * NSEQ:16 * NSEQ], ident[:, :])
                    e.transpose(bkA.ap[0:112, P:2 * P], vs_[:, 16 * NSEQ:23 * NSEQ], ident[:, :])
                    return e.transpose(bkA.ap[0:32, 2 * P:3 * P], us_[:, 8 * NSEQ:10 * NSEQ], ident[:, :])
                pe_group(fno, reads=(Rv, Ru, R_ident), writes=(bkA,))
                bkB = banks.alloc()

                def fno2(e, bkB=bkB, ve=ve, ue=ue, NP=NP):
                    e.transpose(bkB.ap[0:15, 0:P], ve[:, NP:NP + 15], ident[:, :])
                    return e.transpose(bkB.ap[0:2, P:2 * P], ue[:, NP:NP + 2], ident[:, :])
                pe_group(fno2, reads=(Rv, Ru, R_ident), writes=(bkB,))
                so = tmpF.alloc()
                op("act", lambda e, so=so, bkA=bkA: e.activation(out=so.ap[:, 0:3 * P], in_=bkA.ap[:, 0:3 * P], func=AF.Copy),
                   reads=(bkA,), writes=(so,))
                so2 = tmpF.alloc()
                op("act", lambda e, so2=so2, bkB=bkB: e.activation(out=so2.ap[0:15, 0:2 * P], in_=bkB.ap[0:15, 0:2 * P], func=AF.Copy),
                   reads=(bkB,), writes=(so2,))
                jcs = slice(j * P, (j + 1) * P)
                npv = nps.rearrange("(s r) d -> r s d", r=15)
                ncv = ncs.rearrange("(s t) d -> t s d", t=2)
                dma("sp", lambda e, so=so, jcs=jcs, npv=npv: e.dma_start(out=npv[0:8, :, jcs], in_=so.ap[0:128, 0:P]),
                    reads=(so,), track=so, final=True)
                dma("sp", lambda e, so=so, jcs=jcs, npv=npv: e.dma_start(out=npv[8:15, :, jcs], in_=so.ap[0:112, P:2 * P]),
                    reads=(so,), track=so, final=True)
                dma("sp", lambda e, so=so, jcs=jcs, ncv=ncv: e.dma_start(out=ncv[:, :, jcs], in_=so.ap[0:32, 2 * P:3 * P]),
                    reads=(so,), track=so, final=True)
                dma("sp", lambda e, so2=so2, jcs=jcs: e.dma_start(out=npp[0:15, jcs], in_=so2.ap[0:15, 0:P]),
                    reads=(so2,), track=so2, final=True)
                dma("sp", lambda e, so2=so2, jcs=jcs: e.dma_start(out=ncp[0:2, jcs], in_=so2.ap[0:2, P:2 * P]),
                    reads=(so2,), track=so2, final=True)

            pre_chunk(0, stgs.get(0))
            pending_final = None
            for j in range(KC):
                jp = j // 2
                if j % 2 == 0:
                    wv = [ws.view(n) for n in W1a[jp]]
                if has_s and j + 1 < KC:
                    stgs[j + 1] = state_in_dma(j + 1)
                mm_chunk(j, wv)
                if pending_final is not None:
                    pending_final()
                pending_final = pool_chunk(j)
                if has_s and j >= 1:
                    state_out(j - 1)
                if j + 1 < KC:
                    pre_chunk(j + 1, stgs.get(j + 1))
                if j % 2 == 1:
                    for n_ in W1a[jp]:
                        ws.finish(n_)
            pending_final()
            if has_s:
                state_out(KC - 1)

            for jp in range(4):
                (woc, rwoc), (wga, rga), (wgb, rgb) = [ws.view(n) for n in W1b[jp]]
                for j in (2 * jp, 2 * jp + 1):
                    jc = (j % 2) * P
                    g = j // 2
                    for ti, (kind, c0, c1) in enumerate(tiles):
                        n = c1 - c0
                        xr = [R_xn[k][ti] for k in range(KC)]
                        Bya = banks.alloc()
                        mm_group(Bya, [(woc[:, k, jc:jc + P], ua[:, k, c0:c1]) for k in range(KC)],
                                 reads=[R_ua[k][ti] for k in range(KC)] + [rwoc], n=n)
                        Bga = banks.alloc()
                        mm_group(Bga, [(wga[:, k, jc:jc + P], xn[:, k, c0:c1]) for k in range(KC)], reads=xr + [rga], n=n)
                        Byb = banks.alloc()
                        mm_group(Byb, [(wpool_sb[:, g, k, jc:jc + P], pooled[:, 2 * g + k, c0:c1]) for k in range(2)],
                                 reads=[R_pl[2 * g][ti], R_pl[2 * g + 1][ti], R_wpool], n=n)
                        Bgb = banks.alloc()
                        mm_group(Bgb, [(wgb[:, k, jc:jc + P], xn[:, k, c0:c1]) for k in range(KC)], reads=xr + [rgb], n=n)
                        ga = tmpF.alloc()
                        op("act", lambda e, ga=ga, Bga=Bga, n=n: e.activation(out=ga.ap[:, 0:n], in_=Bga.ap[:, 0:n], func=AF.Sigmoid),
                           reads=(Bga,), writes=(ga,))
                        gb = tmpF.alloc()
                        op("act", lambda e, gb=gb, Bgb=Bgb, n=n: e.activation(out=gb.ap[:, 0:n], in_=Bgb.ap[:, 0:n], func=AF.Sigmoid),
                           reads=(Bgb,), writes=(gb,))
                        t1 = tmpF.alloc()
                        op("dve", lambda e, t1=t1, ga=ga, Bya=Bya, n=n: e.tensor_tensor(
                            out=t1.ap[:, 0:n], in0=Bya.ap[:, 0:n], in1=ga.ap[:, 0:n], op=ALU.mult),
                           reads=(Bya, ga), writes=(t1,))
                        op("dve", lambda e, gb=gb, Byb=Byb, n=n, j=j: e.scalar_tensor_tensor(
                            out=Byb.ap[:, 0:n], in0=Byb.ap[:, 0:n], scalar=cv(CV_PSCALE + j), in1=gb.ap[:, 0:n],
                            op0=ALU.mult, op1=ALU.mult),
                           reads=(gb, R_cvec), writes=(Byb,))
                        op("dve", lambda e, t1=t1, Byb=Byb, n=n, j=j, c0=c0, c1=c1: e.tensor_tensor(
                            out=mg[:, j, c0:c1], in0=t1.ap[:, 0:n], in1=Byb.ap[:, 0:n], op=ALU.add),
                           reads=(t1, Byb), writes=(R_mg[j][ti],))
                for n_ in W1b[jp]:
                    ws.finish(n_)

            if st + 1 < len(STS):
                prefetch_inputs(st + 1, "ua")
            wov = [ws.view(n) for n in W2]
            for ti, (kind, c0, c1) in enumerate(tiles):
                n = c1 - c0
                for i in range(KC):
                    wo, rwo = wov[i // 2]
                    ic = (i % 2) * P
                    bk = banks.alloc()
                    mm_group(bk, [(wo[:, k, ic:ic + P], mg[:, k, c0:c1]) for k in range(KC)],
                             reads=[R_mg[k][ti] for k in range(KC)] + [rwo], n=n)
                    op("dve", lambda e, bk=bk, n=n, i=i, c0=c0, c1=c1: e.tensor_tensor(
                        out=hT[:, i, c0:c1], in0=bk.ap[:, 0:n], in1=hT[:, i, c0:c1], op=ALU.add),
                       reads=(bk,), writes=(R_hT[i][ti],))
                norm_deferred(st, ti, CV_GMLP, ("n2", ti))
            for n_ in W2:
                ws.finish(n_)
            if st + 1 < len(STS):
                prefetch_inputs(st + 1, "mg")

            units = [(q, ti) for q in range(4) for ti in range(ntile)]
            LA = 3
            acts = {}
            n_up = [0]

            def emit_up(idx):
                u, f = divmod(idx, 8)
                q, ti = units[u]
                _, c0, c1 = tiles[ti]
                n = c1 - c0
                ensure_t(ti)
                wu, rwu = ws.view(W4[q][0][f // 2])
                fc = (f % 2) * P
                bk = banks.alloc()
                mm_group(bk, [(wu[:, k, fc:fc + P], xn[:, k, c0:c1]) for k in range(KC)],
                         reads=[R_xn[k][ti] for k in range(KC)] + [rwu], n=n)
                rl = tmpF.alloc()
                op("act", lambda e, rl=rl, bk=bk, n=n: e.activation(out=rl.ap[:, 0:n], in_=bk.ap[:, 0:n], func=AF.Relu),
                   reads=(bk,), writes=(rl,))
                a = tmpB.alloc()
                op("act", lambda e, a=a, rl=rl, n=n: e.activation(out=a.ap[:, 0:n], in_=rl.ap[:, 0:n], func=AF.Square),
                   reads=(rl,), writes=(a,))
                acts.setdefault(u, []).append(a)

            for u, (q, ti) in enumerate(units):
                _, c0, c1 = tiles[ti]
                n = c1 - c0
                while n_up[0] < min((u + 1) * 8 + LA, len(units) * 8):
                    emit_up(n_up[0])
                    n_up[0] += 1
                for j in range(KC):
                    wd, rwd = ws.view(W4[q][1][j // 2])
                    jc = (j % 2) * P
                    bk = banks.alloc()
                    mm_group(bk, [(wd[:, f, jc:jc + P], acts[u][f].ap[:, 0:n]) for f in range(8)],
                             reads=acts[u] + [rwd], n=n)
                    op("dve", lambda e, bk=bk, n=n, j=j, c0=c0, c1=c1: e.tensor_tensor(
                        out=hT[:, j, c0:c1], in0=bk.ap[:, 0:n], in1=hT[:, j, c0:c1], op=ALU.add),
                       reads=(bk,), writes=(R_hT[j][ti],))
                if q == 3:
                    norm_deferred(st, ti, CV_GPLE, ("n3", ti), trickle=True)
                if ti == ntile - 1:
                    for n_ in W4[q][0] + W4[q][1]:
                        ws.finish(n_)

            wgv = [ws.view(n) for n in W5]
            for ti, (kind, c0, c1) in enumerate(tiles):
                n = c1 - c0
                ensure_t(ti)
                for j in range(KC):
                    wg, rwg = wgv[j // 2]
                    jc = (j % 2) * P
                    Bg = banks.alloc()
                    mm_group(Bg, [(wg[:, k, jc:jc + P], xn[:, k, c0:c1]) for k in range(KC)],
                             reads=[R_xn[k][ti] for k in range(KC)] + [rwg], n=n)
                    Bp = banks.alloc()
                    mm_group(Bp, [(pproj_sb[:, k, j * P:(j + 1) * P], pT[:, k, c0:c1]) for k in range(2)],
                             reads=[R_pT[ti], R_pproj], n=n)
                    gp = tmpF.alloc()
                    op("act", lambda e, gp=gp, Bg=Bg, n=n: e.activation(out=gp.ap[:, 0:n], in_=Bg.ap[:, 0:n], func=AF.Sigmoid),
                       reads=(Bg,), writes=(gp,))
                    op("dve", lambda e, gp=gp, Bp=Bp, n=n: e.tensor_tensor(
                        out=Bp.ap[:, 0:n], in0=Bp.ap[:, 0:n], in1=gp.ap[:, 0:n], op=ALU.mult),
                       reads=(gp,), writes=(Bp,))
                    op("dve", lambda e, Bp=Bp, n=n, j=j, c0=c0, c1=c1: e.tensor_tensor(
                        out=hT[:, j, c0:c1], in0=Bp.ap[:, 0:n], in1=hT[:, j, c0:c1], op=ALU.add),
                       reads=(Bp,), writes=(R_hT[j][ti],))
                final_norm_deferred(st, ti, ("n4", ti))
            for n_ in W5:
                ws.finish(n_)

            def store_tile(st, ti, p0=p0, tiles=tiles, NPs=NP):
                kind, c0, c1 = tiles[ti]
                ensure_t(ti)
                for b0 in range(c0, c1, P):
                    if b0 < NPs:
                        ydst = yp[p0 + b0:p0 + b0 + P, :]
                        ysl = lambda yd, half: yd[:, half * 512:(half + 1) * 512]
                    else:
                        ydst = ys.rearrange("(s t) d -> t s d", t=DSEQ)
                        ysl = lambda yd, half: yd[:, :, half * 512:(half + 1) * 512]
                    for half in range(2):
                        bk = banks.alloc()

                        def fny(e, bk=bk, half=half, b0=b0):
                            last = None
                            for jj in range(4):
                                j = half * 4 + jj
                                last = e.transpose(bk.ap[:, jj * P:(jj + 1) * P], hT[:, j, b0:b0 + P], ident[:, :])
                            return last
                        pe_group(fny, reads=[R_hT[half * 4 + jj][ti] for jj in range(4)] + [R_ident], writes=(bk,))
                        yo = yor.alloc()
                        bi = (b0 - c0) // P
                        op("dve", lambda e, bk=bk, yo=yo, half=half, bi=bi, ti=ti: e.scalar_tensor_tensor(
                            out=yo.ap[:, :], in0=bk.ap[:, :], scalar=rtok[:, ti, bi:bi + 1],
                            in1=g_bc[:, half * 512:(half + 1) * 512], op0=ALU.mult, op1=ALU.mult),
                           reads=(bk, R_rtok[ti], R_gbc), writes=(yo,))
                        dma("sp", lambda e, yo=yo, ydst=ydst, half=half, ysl=ysl: e.dma_start(
                            out=ysl(ydst, half), in_=yo.ap[:, :]),
                            reads=(yo,), track=yo, final=True)

            if st == 0:
                setup_gbc()
            for ti in range(ntile - 1):
                store_tile(st, ti)
            if st < len(STS) - 1:
                ensure_t(ntile - 1)
                stash.append(lambda st=st, ti=ntile - 1, f=store_tile: f(st, ti))
            else:
                store_tile(st, ntile - 1)
            flush_all()

        assert ws.loaded == len(ws.specs)
        spq = pg.eng["sp"]
        for (s, v) in pg.final_events.values():
            spq.q.append(lambda eng, s=s, v=v: eng.wait_ge(s, v))

        with nc.Block() as block:
            @block.tensor
            def _(e):
                for f in pg.eng["pe"].q:
                    f(e)

            @block.scalar
            def _(e):
                for f in pg.eng["act"].q:
                    f(e)

            @block.vector
            def _(e):
                for f in pg.eng["dve"].q:
                    f(e)

            @block.gpsimd
            def _(e):
                for f in pg.eng["pool"].q:
                    f(e)

            @block.sync
            def _(e):
                for f in pg.eng["sp"].q:
                    f(e)
    return nc


_NC_CACHE = {}


def kernel(x_prompt, x_sample, state_conv, state_pool, p_prompt, p_sample, g_mix, w_in,
           w_conv, w_out_conv, w_pool, pool_scale, w_o, g_mlp, w_up, w_down, g_ple,
           w_ple_gate, w_ple_proj, g_final):
    f = lambda a: np.ascontiguousarray(np.asarray(a, dtype=np.float32))
    if "nc" not in _NC_CACHE:
        _NC_CACHE["nc"] = build_program()
    nc = _NC_CACHE["nc"]
    shared = {
        "g_mix": f(g_mix).reshape(1, D), "g_mlp": f(g_mlp).reshape(1, D), "g_ple": f(g_ple).reshape(1, D),
        "g_final": f(g_final).reshape(1, D), "pool_scale": f(pool_scale).reshape(1, D),
        "w_conv": f(w_conv).reshape(3, D), "w_in": f(w_in).reshape(D, 6 * D),
        "w_out_conv": f(w_out_conv).reshape(D, D), "w_pool": f(w_pool).reshape(4, 256, 256),
        "w_o": f(w_o).reshape(D, D), "w_up": f(w_up).reshape(D, DFF), "w_down": f(w_down).reshape(DFF, D),
        "w_ple_gate": f(w_ple_gate).reshape(D, D), "w_ple_proj": f(w_ple_proj).reshape(PLE, D),
    }
    xpf, xsf = f(x_prompt), f(x_sample)
    ppf, psf = f(p_prompt)[0], f(p_sample)[0]
    scf, spf = f(state_conv)[0], f(state_pool)[0]
    in_maps = []
    for c in range(NCORE):
        m = dict(shared)
        sl = slice(c * NSEQ, (c + 1) * NSEQ)
        m["xp"] = xpf[c]
        m["xs"] = xsf[sl].reshape(NSAMP, D)
        m["pp"] = ppf[c]
        m["ps"] = psf[sl].reshape(NSAMP, PLE)
        m["sc"] = scf[sl].reshape(NSEQ * 2, D)
        m["sp"] = spf[sl].reshape(NSEQ * 15, D)
        in_maps.append(m)
    res = run_bass_kernel_spmd(nc, in_maps, core_ids=list(range(NCORE)))
    r = res.results
    y_prompt = np.stack([r[c]["yp"] for c in range(NCORE)]).astype(np.float32)
    y_sample = np.concatenate([r[c]["ys"].reshape(NSEQ, DSEQ, D) for c in range(NCORE)]).astype(np.float32)
    ncp_ = np.stack([r[c]["ncp"] for c in range(NCORE)])[None].astype(np.float32)
    npp_ = np.stack([r[c]["npp"] for c in range(NCORE)])[None].astype(np.float32)
    ncs_ = np.concatenate([r[c]["ncs"].reshape(NSEQ, 2, D) for c in range(NCORE)])[None].astype(np.float32)
    nps_ = np.concatenate([r[c]["nps"].reshape(NSEQ, 15, D) for c in range(NCORE)])[None].astype(np.float32)
    return (y_prompt, y_sample, ncp_, npp_, ncs_, nps_)
```
